# Optimizing a Trainium2 kernel written in Bass

```python
import math
import jax, jax.numpy as jnp
from jax import lax
import numpy as np

D_MODEL = 1024
BATCH = 8
SEQ = 2048
DEPTH = 4

N_MIXERS = 2
N_HEADS = 16
HEAD_DIM = D_MODEL // N_HEADS
CONV_WIDTH = 3
BRANCHES = ((128, 1), (512, 4), (2048, 16))
NUM_BUCKETS = 32
MAX_DISTANCE = 2048
D_FF = ((8 * D_MODEL // 3 + 255) // 256) * 256
EPS = 1e-6
NEG_INF = -1e30

kernel_name = "hybrid_shortconv_dilated_attn_swiglu"


def rms_norm(x, g):
    xf = x.astype(jnp.float32)
    y = xf * lax.rsqrt(jnp.mean(xf * xf, axis=-1, keepdims=True) + EPS)
    return (y * g.astype(jnp.float32)).astype(x.dtype)


def t5_bucket(dist):
    exact = NUM_BUCKETS // 2
    df = jnp.maximum(dist, 1).astype(jnp.float32)
    large = exact + (jnp.log(df / exact) / math.log(MAX_DISTANCE / exact)
                     * (NUM_BUCKETS - exact)).astype(jnp.int32)
    large = jnp.minimum(large, NUM_BUCKETS - 1)
    return jnp.where(dist < exact, dist, large)


def short_conv_mixer(h, w_in, conv_k, w_out):
    b, c, u = jnp.split(h @ w_in, 3, axis=-1)
    y = lax.conv_general_dilated(
        c * u, conv_k[:, None, :].astype(u.dtype),
        window_strides=(1,), padding=[(CONV_WIDTH - 1, 0)],
        dimension_numbers=("NWC", "WIO", "NWC"),
        feature_group_count=D_MODEL)
    return (b * y) @ w_out


def dilated_branch(q, k, v, rel_bias, window, dilation):
    bsz, seq, nh, dh = q.shape
    blk = window // dilation
    L = seq // dilation
    nb = -(-L // blk)
    Lp = nb * blk

    def sub(t):
        t = t.reshape(bsz, L, dilation, nh, dh).transpose(0, 2, 1, 3, 4)
        t = jnp.pad(t, ((0, 0), (0, 0), (0, Lp - L), (0, 0), (0, 0)))
        return t.reshape(bsz, dilation, nb, blk, nh, dh)

    def with_prev(t):
        prev = jnp.pad(t, ((0, 0), (0, 0), (1, 0), (0, 0), (0, 0), (0, 0)))[:, :, :-1]
        return jnp.concatenate([prev, t], axis=3)

    qb = sub(q)
    kk = with_prev(sub(k))
    vv = with_prev(sub(v))

    s = jnp.einsum("brnqhd,brnkhd->brnhqk", qb, kk,
                   preferred_element_type=jnp.float32) * (HEAD_DIM ** -0.5)
    qi = jnp.arange(blk)[:, None]
    ki = jnp.arange(2 * blk)[None, :]
    rel = qi + blk - ki
    band = (rel >= 0) & (rel <= blk)
    valid_start = (jnp.arange(nb)[:, None, None] > 0) | (ki >= blk)[None]
    mask = band[None] & valid_start
    bias = rel_bias[t5_bucket(jnp.clip(rel, 0) * dilation)]
    s = s + bias.transpose(2, 0, 1).astype(jnp.float32)
    s = jnp.where(mask[:, None], s, NEG_INF)
    lse = jax.nn.logsumexp(s, axis=-1)
    p = jnp.exp(s - lse[..., None])
    o = jnp.einsum("brnhqk,brnkhd->brnqhd", p.astype(v.dtype), vv,
                   preferred_element_type=jnp.float32)

    o = o.reshape(bsz, dilation, Lp, nh, dh)[:, :, :L]
    o = o.transpose(0, 2, 1, 3, 4).reshape(bsz, seq, nh, dh)
    lse = lse.transpose(0, 1, 2, 4, 3).reshape(bsz, dilation, Lp, nh)[:, :, :L]
    lse = lse.transpose(0, 2, 1, 3).reshape(bsz, seq, nh)
    return o, lse


def dilated_attention_mixer(h, w_qkv, w_out, rel_bias):
    bsz, seq, _ = h.shape
    qkv = (h @ w_qkv).reshape(bsz, seq, 3, N_HEADS, HEAD_DIM)
    q, k, v = qkv[:, :, 0], qkv[:, :, 1], qkv[:, :, 2]
    outs, lses = [], []
    for window, dilation in BRANCHES:
        o, l = dilated_branch(q, k, v, rel_bias, window, dilation)
        outs.append(o)
        lses.append(l)
    alpha = jax.nn.softmax(jnp.stack(lses, axis=0), axis=0)
    o = jnp.sum(alpha[..., None] * jnp.stack(outs, axis=0), axis=0)
    return o.reshape(bsz, seq, D_MODEL).astype(h.dtype) @ w_out


def swiglu(h, w_gate, w_up, w_down):
    return (jax.nn.silu(h @ w_gate) * (h @ w_up)) @ w_down


def setup_inputs(seed: int = 0) -> dict:
    key = jax.random.key(seed)
    ks = jax.random.split(key, 14)
    n_conv = (DEPTH + 1) // 2
    n_attn = DEPTH // 2
    f32 = jnp.float32
    nrm = lambda k, shape, scale: jax.random.normal(k, shape, f32) * scale
    return {
        "x": nrm(ks[0], (BATCH, SEQ, D_MODEL), 1.0),
        "mix_norm": 1.0 + nrm(ks[1], (DEPTH, D_MODEL), 0.05),
        "ffn_norm": 1.0 + nrm(ks[2], (DEPTH, D_MODEL), 0.05),
        "final_norm": 1.0 + nrm(ks[3], (D_MODEL,), 0.05),
        "conv_w_in": nrm(ks[4], (n_conv, D_MODEL, 3 * D_MODEL), D_MODEL ** -0.5),
        "conv_kernel": nrm(ks[5], (n_conv, CONV_WIDTH, D_MODEL), CONV_WIDTH ** -0.5),
        "conv_w_out": nrm(ks[6], (n_conv, D_MODEL, D_MODEL), D_MODEL ** -0.5),
        "attn_w_qkv": nrm(ks[7], (n_attn, D_MODEL, 3 * D_MODEL), D_MODEL ** -0.5),
        "attn_w_out": nrm(ks[8], (n_attn, D_MODEL, D_MODEL), D_MODEL ** -0.5),
        "rel_bias": nrm(ks[9], (NUM_BUCKETS, N_HEADS), 0.2),
        "ffn_w_gate": nrm(ks[10], (DEPTH, D_MODEL, D_FF), D_MODEL ** -0.5),
        "ffn_w_up": nrm(ks[11], (DEPTH, D_MODEL, D_FF), D_MODEL ** -0.5),
        "ffn_w_down": nrm(ks[12], (DEPTH, D_FF, D_MODEL), D_FF ** -0.5),
    }


def reference(x, mix_norm, ffn_norm, final_norm, conv_w_in, conv_kernel, conv_w_out,
              attn_w_qkv, attn_w_out, rel_bias, ffn_w_gate, ffn_w_up, ffn_w_down):
    for i in range(DEPTH):
        h = rms_norm(x, mix_norm[i])
        j = i // N_MIXERS
        if i % N_MIXERS == 0:
            x = x + short_conv_mixer(h, conv_w_in[j], conv_kernel[j], conv_w_out[j])
        else:
            x = x + dilated_attention_mixer(h, attn_w_qkv[j], attn_w_out[j], rel_bias)
        h = rms_norm(x, ffn_norm[i])
        x = x + swiglu(h, ffn_w_gate[i], ffn_w_up[i], ffn_w_down[i])
    return rms_norm(x, final_norm)
```

```python
import math
import numpy as np
import concourse.bass as bass
import concourse.mybir as mybir
from concourse.bass_utils import run_bass_kernel_spmd

F32 = mybir.dt.float32
BF16 = mybir.dt.bfloat16
AF = mybir.ActivationFunctionType
ALU = mybir.AluOpType

D = 1024
SEQ = 2048
DFF = 2816
NL = 4
NT = 4
TW = 512
KC = 8
FG = 11
GL = 384
WSLOT = 3072
NSLOT = 3

ENGS = ("pe", "act", "dve", "pool", "sp")
SEM_K = 6
SEM_CH = 512


class Op:
    __slots__ = ("eng", "fn", "idx", "rdeps", "wdeps", "is_dma", "sem_key", "val",
                 "needs_signal", "sig", "waits", "nstarts")


class Sched:
    def __init__(self):
        self.ops = {e: [] for e in ENGS}
        self.lastw = {}
        self.readers = {}
        self.dma_count = {}

    def _add(self, eng, fn, reads, writes, is_dma=False, sem_key=None, nstarts=1):
        o = Op()
        o.eng = eng
        o.fn = fn
        o.idx = len(self.ops[eng])
        o.is_dma = is_dma
        o.sem_key = sem_key
        o.needs_signal = False
        o.sig = -1
        o.nstarts = nstarts
        o.val = 0
        if is_dma:
            c = self.dma_count.get(sem_key, 0) + nstarts
            self.dma_count[sem_key] = c
            o.val = 16 * c
        rdeps = []
        wdeps = []
        for r in reads:
            w = self.lastw.get(r)
            if w is not None:
                rdeps.append(w)
        for r in writes:
            w = self.lastw.get(r)
            if w is not None:
                wdeps.append(w)
            rd = self.readers.get(r)
            if rd:
                for x in rd.values():
                    if isinstance(x, list):
                        wdeps.extend(x)
                    else:
                        wdeps.append(x)
        for r in reads:
            d = self.readers.setdefault(r, {})
            if is_dma:
                d.setdefault("dma", []).append(o)
            else:
                d[eng] = o
        for r in writes:
            self.lastw[r] = o
            self.readers[r] = {}
        o.rdeps = rdeps
        o.wdeps = wdeps
        self.ops[eng].append(o)
        return o

    def op(self, eng, fn, reads=(), writes=()):
        return self._add(eng, fn, reads, writes)

    def dma(self, queue, fn, reads=(), writes=(), sem_key=None, nstarts=1):
        return self._add(queue, fn, reads, writes, True, sem_key, nstarts)

    def resolve(self):
        for e in ENGS:
            seen_idx = {x: -1 for x in ENGS}
            seen_dma = {}
            for o in self.ops[e]:
                best = {}
                final = []
                deps = [(d, True) for d in o.rdeps] + [(d, False) for d in o.wdeps]
                for d, is_raw in deps:
                    if d is o:
                        continue
                    if d.is_dma:
                        if seen_dma.get(d.sem_key, 0) >= d.val:
                            continue
                        seen_dma[d.sem_key] = d.val
                        final.append(d)
                        continue
                    if d.eng == e:
                        if e == "pe":
                            continue
                        if not (is_raw or o.is_dma):
                            continue
                    if seen_idx[d.eng] >= d.idx:
                        continue
                    seen_idx[d.eng] = d.idx
                    b = best.get(d.eng)
                    if b is None or d.idx > b.idx:
                        best[d.eng] = d
                for d in best.values():
                    d.needs_signal = True
                    final.append(d)
                o.waits = final
        for e in ENGS:
            n = 0
            for o in self.ops[e]:
                if o.needs_signal and not o.is_dma:
                    o.sig = n
                    n += 1

    @staticmethod
    def sem_slot_val(sig):
        c = sig // SEM_CH
        slot = c % SEM_K
        val = (c // SEM_K) * SEM_CH + (sig % SEM_CH) + 1
        return slot, val

    def emit(self, nc, final_wait_keys=()):
        self.resolve()
        import contextlib
        with contextlib.ExitStack() as st:
            esems = {}
            for e in ENGS:
                esems[e] = [st.enter_context(nc.semaphore(f"s_{e}_{i}")) for i in range(SEM_K)]
            dsems = {}
            for k in self.dma_count:
                dsems[k] = st.enter_context(nc.semaphore(f"d_{k}"))
            block = st.enter_context(nc.Block())

            def run(e, eng):
                for o in self.ops[e]:
                    for d in o.waits:
                        if d.is_dma:
                            eng.wait_ge(dsems[d.sem_key], d.val)
                        else:
                            slot, val = self.sem_slot_val(d.sig)
                            eng.wait_ge(esems[d.eng][slot], val)
                    r = o.fn(eng)
                    if o.is_dma:
                        if not isinstance(r, (list, tuple)):
                            r = [r]
                        assert len(r) == o.nstarts, (len(r), o.nstarts)
                        for ins in r:
                            ins.then_inc(dsems[o.sem_key], 16)
                    elif o.needs_signal:
                        slot, _ = self.sem_slot_val(o.sig)
                        r.then_inc(esems[e][slot], 1)
                if e == "sp":
                    for k in final_wait_keys:
                        eng.wait_ge(dsems[k], 16 * self.dma_count[k])

            @block.tensor
            def _(eng):
                run("pe", eng)

            @block.scalar
            def _(eng):
                run("act", eng)

            @block.vector
            def _(eng):
                run("dve", eng)

            @block.gpsimd
            def _(eng):
                run("pool", eng)

            @block.sync
            def _(eng):
                run("sp", eng)


R_O = 0
R_Q = 16384
R_K = 18432
R_KB = 20480
R_V = 22528
R_VT = 24576
R_E = 33792
NP1 = 3
NPP = 5
R_P1 = 35072
R_P = R_P1 + NP1 * 512
R_N = R_P + NPP * 512


class Builder:
    def __init__(self, layers=(0, 1, 2, 3), final=True):
        self.layers = tuple(layers)
        self.final = final
        nc = self.nc = bass.Bass("TRN2", target_bir_lowering=False)
        self.S = Sched()
        dt = nc.dram_tensor
        self.xd = dt("x", [128, KC * SEQ], F32, kind="ExternalInput").ap()
        self.gd = dt("gains", [128, 72], F32, kind="ExternalInput").ap()
        self.ckd = dt("ck", [128, 48], F32, kind="ExternalInput").ap()
        self.idd = dt("ident", [128, 128], F32, kind="ExternalInput").ap()
        self.ohd = dt("oh", [33, 3 * GL], F32, kind="ExternalInput").ap()
        self.rbd = dt("rbx", [33, 16], F32, kind="ExternalInput").ap()
        self.wmix = {}
        self.wout = {}
        self.wgu = {}
        self.wdn = {}
        for l in self.layers:
            self.wmix[l] = dt(f"wmix{l}", [8, 128, 3072], F32, kind="ExternalInput").ap()
            self.wout[l] = dt(f"wout{l}", [4, 128, 2048], F32, kind="ExternalInput").ap()
            self.wgu[l] = dt(f"wgu{l}", [22, 128, 2048], F32, kind="ExternalInput").ap()
            self.wdn[l] = dt(f"wdn{l}", [8, 128, 2816], F32, kind="ExternalInput").ap()
        self.yd = dt("y", [128, KC * SEQ], F32, kind="ExternalOutput").ap()
        self.gscr = dt("gscr", [128, 48 * GL], BF16).ap()

        a = nc.alloc_sbuf_tensor
        self.xT = a("xT", [128, KC * SEQ], F32)
        self.hT = a("hT", [128, KC * SEQ], BF16)
        self.R = a("R", [128, R_N], BF16)
        self.wsl = [a(f"wsl{i}", [128, WSLOT], BF16) for i in range(NSLOT)]
        self.rs = [a(f"rs{i}", [128, TW], F32) for i in range(2)]
        self.tmp = [a(f"tmp{i}", [128, TW], F32) for i in range(4)]
        self.sq = [a(f"sq{i}", [128, TW], BF16) for i in range(2)]
        self.gains = a("gains_sb", [128, 72], F32)
        self.bar = a("bar_sb", [128, 20], F32)
        self.tab = a("tab_sb", [128, 1152], BF16)
        self.ck = a("ck_sb", [128, 48], F32)
        self.ident = a("ident_sb", [128, 128], BF16)
        self.ones = a("ones_sb", [128, 128], BF16)
        self.pb = [nc.alloc_psum_tensor(f"pb{i}", [128, TW], F32) for i in range(7)]
        self.pt = nc.alloc_psum_tensor("pt", [128, 1024], BF16)

        print("sbuf remaining", nc.sbuf_bytes_remaining)
        self.out_keys = []
        self.side_jobs = []
        self.wlist = []
        self.wissued = 0
        self.wptr = 0
        self.cnt = 0

    def xs(self, c, t):
        return self.xT[:, c * SEQ + t * TW: c * SEQ + (t + 1) * TW]

    def hs(self, c, t):
        return self.hT[:, c * SEQ + t * TW: c * SEQ + (t + 1) * TW]

    def os_(self, c, t):
        return self.R[:, R_O + c * SEQ + t * TW: R_O + c * SEQ + (t + 1) * TW]

    def mm(self, out, lhsT, rhs, start, stop, reads, writes, skip=False):
        if skip:
            fn = lambda e: e.matmul(out, lhsT=lhsT, rhs=rhs, start=start, stop=stop, skip_group_check=True)
        else:
            fn = lambda e: e.matmul(out, lhsT=lhsT, rhs=rhs, start=start, stop=stop)
        return self.S.op("pe", fn, reads=reads, writes=writes)

    def plan_weights(self):
        for l in self.layers:
            for j in range(8):
                self.wlist.append((self.wmix[l][j], 3072))
            for half in range(2):
                for i in range(4):
                    self.wlist.append((self.wout[l][i], 2048))
            for g in range(2):
                for fl in range(FG):
                    self.wlist.append((self.wgu[l][g * FG + fl], 2048))
                for half in range(2 if g == 1 else 1):
                    for i in range(4):
                        self.wlist.append((self.wdn[l][g * 4 + i], 2816))

    def _issue_w(self, i):
        src, L = self.wlist[i]
        slot = i % NSLOT
        dst = self.wsl[slot][:, 0:L]
        self.S.dma("pool", lambda e: e.dma_start(out=dst, in_=src), writes=[f"w{slot}"], sem_key=f"w{slot}")

    def acquire(self):
        i = self.wptr
        while self.wissued < min(len(self.wlist), i + NSLOT):
            self._issue_w(self.wissued)
            self.wissued += 1
        self.wptr += 1
        slot = i % NSLOT
        return self.wsl[slot], f"w{slot}"

    def prologue(self):
        S = self.S
        S.dma("sp", lambda e: e.dma_start(out=self.gains[:], in_=self.gd), writes=["gains"], sem_key="misc")
        S.dma("sp", lambda e: e.dma_start(out=self.ck[:], in_=self.ckd), writes=["ck"], sem_key="misc2")
        S.dma("pool", lambda e: e.dma_start(out=self.ident[:], in_=self.idd), writes=["ident"], sem_key="ident")
        S.op("dve", lambda e: e.memset(self.ones[:], 1.0), writes=["ones"])
        xv = self.xT[:].rearrange("p (c t w) -> p c t w", c=KC, t=NT)
        xdv = self.xd.rearrange("p (c t w) -> p c t w", c=KC, t=NT)
        for t in range(NT):
            dst = xv[:, :, t, :]
            src = xdv[:, :, t, :]
            S.dma("sp", lambda e, dst=dst, src=src: e.dma_start(out=dst, in_=src),
                  writes=[f"x{c}_{t}" for c in range(KC)], sem_key=f"xl{t}")

    def build_tables(self):
        S = self.S
        G = self.R[:, R_Q:R_Q + 48 * GL]
        RBs = [self.rs[0][:].bitcast(BF16), self.rs[1][:].bitcast(BF16)]
        RBX = self.bar[:, 4:20]
        OH = self.tab[:, 0:1152]
        S.dma("pool", lambda e: e.dma_start(out=OH[0:33, :], in_=self.ohd), writes=["OH"], sem_key="oh")
        S.dma("sp", lambda e: e.dma_start(out=RBX[0:33, :], in_=self.rbd), writes=["RBX"], sem_key="rbx")
        for h in range(16):
            dst = RBs[h // 8][0:33, (h % 8) * 128:(h % 8 + 1) * 128]
            src = RBX[0:33, h:h + 1].to_broadcast([33, 128])
            S.op("pool", lambda e, dst=dst, src=src: e.tensor_copy(out=dst, in_=src),
                 reads=["RBX"], writes=[f"rs{h // 8}"])
        def job(h, br, n):
            def run():
                b = 6
                ps = self.pb[b][:, 0:GL]
                self.mm(ps, RBs[h // 8][0:33, (h % 8) * 128:(h % 8 + 1) * 128], OH[0:33, br * GL:(br + 1) * GL],
                        True, True, reads=[f"rs{h // 8}", "OH"], writes=[f"pb{b}"])
                dst = G[:, n * GL:(n + 1) * GL]
                S.op("act", lambda e, dst=dst, ps=ps: e.activation(out=dst, in_=ps, func=AF.Exp),
                     reads=[f"pb{b}"], writes=["G"])
            return run

        def finish():
            self._tables_finish(G)

        jobs = []
        n = 0
        for h in range(16):
            for br in range(3):
                jobs.append(job(h, br, n))
                n += 1
        jobs.append(finish)
        self.side_jobs = jobs

    def run_side(self, k):
        for _ in range(k):
            if self.side_jobs:
                self.side_jobs.pop(0)()

    def _tables_finish(self, G):
        S = self.S
        S.dma("sp", lambda e: e.dma_start(out=self.gscr, in_=G), reads=["G"], writes=["gscr"], sem_key="gscr")
        names = ["G", "Vones", "E", "kzA", "kzB"]
        names += [f"o{c}_{t}" for c in range(8, FG) for t in range(NT)]
        for t in range(NT):
            names += [f"q{t}", f"k{t}", f"kb{t}", f"v{t}"]
        names += [f"V{lay}_{half}" for lay in range(3) for half in range(2)]
        names += [f"P1_{i}" for i in range(NP1)] + [f"P_{i}" for i in range(NPP)]
        S.dma("sp", lambda e: e.dma_start(out=self.bar[0:1, 0:4], in_=self.gd[0:1, 0:4]), writes=names, sem_key="bar")

    def norm(self, gi, final=False):
        S = self.S
        for t in range(NT):
            ss = self.pb[6]
            for c in range(KC):
                sq = self.sq[c % 2]
                xin = self.xs(c, t)
                S.op("act", lambda e, sq=sq, xin=xin: e.activation(out=sq[:], in_=xin, func=AF.Square),
                     reads=[f"x{c}_{t}"], writes=[f"sq{c % 2}"])
                self.mm(ss[:], self.ones[:], sq[:], c == 0, c == KC - 1, reads=[f"sq{c % 2}", "ones"], writes=["pb6"])
            rs = self.rs[t % 2]
            S.op("dve", lambda e, rs=rs, ss=ss: e.tensor_scalar(out=rs[:], in0=ss[:], scalar1=1.0 / D, scalar2=1e-6,
                                                              op0=ALU.mult, op1=ALU.add),
                 reads=["pb6"], writes=[f"rs{t % 2}"])
            S.op("act", lambda e, rs=rs: e.activation(out=rs[:], in_=rs[:], func=AF.Ln),
                 reads=[f"rs{t % 2}"], writes=[f"rs{t % 2}"])
            S.op("act", lambda e, rs=rs: e.activation(out=rs[:], in_=rs[:], func=AF.Exp, scale=-0.5),
                 reads=[f"rs{t % 2}"], writes=[f"rs{t % 2}"])
            for c in range(KC):
                xin = self.xs(c, t)
                g = self.gains[:, gi * 8 + c: gi * 8 + c + 1]
                if final:
                    S.op("dve", lambda e, xin=xin, g=g, rs=rs: e.scalar_tensor_tensor(
                        out=xin, in0=xin, scalar=g, in1=rs[:], op0=ALU.mult, op1=ALU.mult),
                        reads=[f"x{c}_{t}", f"rs{t % 2}", "gains"], writes=[f"x{c}_{t}"])
                    if c == KC - 1:
                        xv = self.xT[:].rearrange("p (c t w) -> p c t w", c=KC, t=NT)[:, :, t, :]
                        yv = self.yd.rearrange("p (c t w) -> p c t w", c=KC, t=NT)[:, :, t, :]
                        S.dma("sp", lambda e, xv=xv, yv=yv: e.dma_start(out=yv, in_=xv),
                              reads=[f"x{cc}_{t}" for cc in range(KC)], sem_key=f"yo{t}")
                        self.out_keys.append(f"yo{t}")
                else:
                    ho = self.hs(c, t)
                    if True:
                        S.op("dve", lambda e, xin=xin, g=g, rs=rs, ho=ho: e.scalar_tensor_tensor(
                            out=ho, in0=xin, scalar=g, in1=rs[:], op0=ALU.mult, op1=ALU.mult),
                            reads=[f"x{c}_{t}", f"rs{t % 2}", "gains"], writes=[f"h{c}_{t}"])
                    else:
                        pt_ = self.ptmp
                        S.op("pool", lambda e, xin=xin, g=g, pt_=pt_: e.tensor_scalar(
                            out=pt_[:], in0=xin, scalar1=g, scalar2=None, op0=ALU.mult),
                            reads=[f"x{c}_{t}", "gains"], writes=["ptmp"])
                        S.op("pool", lambda e, rs=rs, ho=ho, pt_=pt_: e.tensor_tensor(
                            out=ho, in0=pt_[:], in1=rs[:], op=ALU.mult),
                            reads=["ptmp", f"rs{t % 2}"], writes=[f"h{c}_{t}"])

    def out_proj(self):
        S = self.S
        for half, i in [(h_, i_) for h_ in range(2) for i_ in range(4)]:
            w, wr = self.acquire()
            wv = w[:, 0:2048].rearrange("p (nn k n) -> p nn k n", nn=2, k=KC)
            for nn in range(2):
                n = 2 * i + nn
                for t in range(2 * half, 2 * half + 2):
                    b = self.cnt % 5
                    self.cnt += 1
                    ps = self.pb[b]
                    for k in range(KC):
                        self.mm(ps[:], wv[:, nn, k, :], self.os_(k, t), k == 0, k == KC - 1,
                                reads=[wr, f"o{k}_{t}"], writes=[f"pb{b}"])
                    xo = self.xs(n, t)
                    S.op("dve", lambda e, xo=xo, ps=ps: e.tensor_tensor(out=xo, in0=ps[:], in1=xo, op=ALU.add),
                         reads=[f"pb{b}", f"x{n}_{t}"], writes=[f"x{n}_{t}"])

    def ffn(self):
        S = self.S
        for g in range(2):
            for fl in range(FG):
                w, wr = self.acquire()
                wv = w[:, 0:2048].rearrange("p (k r n) -> p k r n", k=KC, r=2)
                for t in range(NT):
                    pr = self.cnt % 2
                    self.cnt += 1
                    bG, bU = 2 * pr, 2 * pr + 1
                    for k in range(KC):
                        self.mm(self.pb[bG][:], wv[:, k, 0, :], self.hs(k, t), k == 0, k == KC - 1,
                                reads=[wr, f"h{k}_{t}"], writes=[f"pb{bG}"])
                    for k in range(KC):
                        self.mm(self.pb[bU][:], wv[:, k, 1, :], self.hs(k, t), k == 0, k == KC - 1,
                                reads=[wr, f"h{k}_{t}"], writes=[f"pb{bU}"])
                    tm = self.tmp[pr]
                    pG = self.pb[bG]
                    pU = self.pb[bU]
                    S.op("act", lambda e, tm=tm, pG=pG: e.activation(out=tm[:], in_=pG[:], func=AF.Silu),
                         reads=[f"pb{bG}"], writes=[f"tmp{pr}"])
                    ao = self.os_(fl, t) if True else None
                    S.op("dve", lambda e, ao=ao, pU=pU, tm=tm: e.tensor_tensor(out=ao, in0=pU[:], in1=tm[:], op=ALU.mult),
                         reads=[f"pb{bU}", f"tmp{pr}"], writes=[f"o{fl}_{t}"])
            passes = [(None, i_) for i_ in range(4)] if g == 0 else [(h_, i_) for h_ in range(2) for i_ in range(4)]
            for half, i in passes:
                w, wr = self.acquire()
                wv = w[:, 0:2816].rearrange("p (nn f n) -> p nn f n", nn=2, f=FG)
                for nn in range(2):
                    n = 2 * i + nn
                    for t in (range(NT) if half is None else range(2 * half, 2 * half + 2)):
                        b = 4 + self.cnt % 2
                        self.cnt += 1
                        ps = self.pb[b]
                        for fl in range(FG):
                            self.mm(ps[:], wv[:, nn, fl, :], self.os_(fl, t), fl == 0, fl == FG - 1,
                                    reads=[wr, f"o{fl}_{t}"], writes=[f"pb{b}"])
                        xo = self.xs(n, t)
                        S.op("dve", lambda e, xo=xo, ps=ps: e.tensor_tensor(out=xo, in0=ps[:], in1=xo, op=ALU.add),
                             reads=[f"pb{b}", f"x{n}_{t}"], writes=[f"x{n}_{t}"])

    def conv_mixer(self, l2):
        S = self.S
        cu = self.R[:, R_N - 4224:R_N].bitcast(F32)
        S.op("dve", lambda e: e.memset(cu[:, 0:2], 0.0), writes=["cuh"])
        for j in range(8):
            w, wr = self.acquire()
            wv = w[:, 0:3072].rearrange("p (k r n) -> p k r n", k=KC, r=3)
            for t in range(NT):
                pr = self.cnt % 2
                self.cnt += 1
                bB, bC, bU = 3 * pr, 3 * pr + 1, 3 * pr + 2
                for r, b in ((0, bB), (1, bC), (2, bU)):
                    for k in range(KC):
                        self.mm(self.pb[b][:], wv[:, k, r, :], self.hs(k, t), k == 0, k == KC - 1,
                                reads=[wr, f"h{k}_{t}"], writes=[f"pb{b}"])
                self.run_side(2)
                tu = self.tmp[pr]
                ty = self.tmp[2 + pr]
                pB, pC, pU = self.pb[bB], self.pb[bC], self.pb[bU]
                S.op("act", lambda e, tu=tu, pU=pU: e.activation(out=tu[:], in_=pU[:], func=AF.Copy),
                     reads=[f"pb{bU}"], writes=[f"tmp{pr}"])
                c0 = cu[:, 2 + t * TW: 2 + (t + 1) * TW]
                c1 = cu[:, 1 + t * TW: 1 + (t + 1) * TW]
                c2 = cu[:, t * TW: (t + 1) * TW]
                S.op("dve", lambda e, c0=c0, pC=pC, tu=tu: e.tensor_tensor(out=c0, in0=pC[:], in1=tu[:], op=ALU.mult),
                     reads=[f"pb{bC}", f"tmp{pr}"], writes=[f"cu{t}"])
                kb = (l2 * 8 + j) * 3
                k0 = self.ck[:, kb:kb + 1]
                k1 = self.ck[:, kb + 1:kb + 2]
                k2 = self.ck[:, kb + 2:kb + 3]
                prev = [f"cu{t - 1}"] if t > 0 else ["cuh"]
                S.op("act", lambda e, ty=ty, c0=c0, k2=k2: e.activation(out=ty[:], in_=c0, func=AF.Copy, scale=k2),
                     reads=[f"cu{t}", "ck"], writes=[f"tmp{2 + pr}"])
                S.op("dve", lambda e, ty=ty, c1=c1, k1=k1: e.scalar_tensor_tensor(out=ty[:], in0=c1, scalar=k1, in1=ty[:],
                                                                                op0=ALU.mult, op1=ALU.add),
                     reads=[f"cu{t}", f"tmp{2 + pr}"] + prev, writes=[f"tmp{2 + pr}"])
                S.op("dve", lambda e, ty=ty, c2=c2, k0=k0: e.scalar_tensor_tensor(out=ty[:], in0=c2, scalar=k0, in1=ty[:],
                                                                                op0=ALU.mult, op1=ALU.add),
                     reads=[f"cu{t}", f"tmp{2 + pr}"] + prev, writes=[f"tmp{2 + pr}"])
                zo = self.os_(j, t)
                S.op("dve", lambda e, zo=zo, pB=pB, ty=ty: e.tensor_tensor(out=zo, in0=pB[:], in1=ty[:], op=ALU.mult),
                     reads=[f"pb{bB}", f"tmp{2 + pr}"], writes=[f"o{j}_{t}"])

    def attn_prep(self):
        S = self.S
        R = self.R
        kT = R[:, R_K:R_K + SEQ]
        kTB = R[:, R_KB:R_KB + SEQ]
        Vt = R[:, R_VT:R_VT + 48 * 192].rearrange("p (b c) -> p b c", c=192)
        S.op("pool", lambda e: e.memset(Vt[:, :, 64:128], 1.0), writes=["Vones"])
        S.op("pool", lambda e: e.memset(kT[64:128, :], 0.0), writes=["kzA"] + [f"o9_{t}" for t in range(NT)])
        S.op("pool", lambda e: e.memset(kTB[0:64, :], 0.0), writes=["kzB"] + [f"o10_{t}" for t in range(NT)])

    def attn_mixer(self):
        S = self.S
        R = self.R
        qT = R[:, R_Q:R_Q + SEQ]
        kT = R[:, R_K:R_K + SEQ]
        kTB = R[:, R_KB:R_KB + SEQ]
        kTs = (kT, kTB)
        vT = R[:, R_V:R_V + SEQ]
        Vt = R[:, R_VT:R_VT + 48 * 192].rearrange("p (b c) -> p b c", c=192)
        Et = R[:, R_E:R_E + 1280].rearrange("p (h c) -> p h c", h=2)
        P1 = [R[:, R_P1 + i * TW: R_P1 + (i + 1) * TW] for i in range(NP1)]
        PP = [R[:, R_P + i * TW: R_P + (i + 1) * TW] for i in range(NPP)]
        q4 = qT.rearrange("p (m r) -> p r m", r=4)
        k4s = [x.rearrange("p (m r) -> p r m", r=4) for x in kTs]
        k16s = [x.rearrange("p (m r) -> p r m", r=16) for x in kTs]
        q16 = qT.rearrange("p (m r) -> p r m", r=16)
        v4 = vT.rearrange("p (m r) -> p r m", r=4)
        v16 = vT.rearrange("p (m r) -> p r m", r=16)
        SK = 48 * GL - 1
        gten = self.gscr.tensor


        for hp in range(8):
            w, wr = self.acquire()
            wv = w[:, 0:3072].rearrange("p (k r n) -> p k r n", k=KC, r=3)

            def eload(e, hp=hp):
                ins = []
                for hh in range(2):
                    h = 2 * hp + hh
                    src = bass.AP(gten, 127 + (h * 3) * GL, [[SK, 128], [GL, 2], [1, 256]])
                    dst = Et[:, hh, 0:512].rearrange("p (b c) -> p b c", b=2)
                    ins.append(e.dma_start(out=dst, in_=src))
                    src = bass.AP(gten, 127 + (h * 3 + 2) * GL, [[SK, 128], [1, 128]])
                    ins.append(e.dma_start(out=Et[:, hh, 512:640], in_=src))
                return ins
            S.dma("sp", eload, reads=["gscr"], writes=["E", "cuh"] + [f"cu{t}" for t in range(NT)],
                  sem_key="E", nstarts=4)

            for t in range(NT):
                for r in range(3):
                    for k in range(KC):
                        self.mm(self.pb[r][:], wv[:, k, r, :], self.hs(k, t), k == 0, k == KC - 1,
                                reads=[wr, f"h{k}_{t}"], writes=[f"pb{r}"])
                sl = slice(t * TW, (t + 1) * TW)
                p0, p1, p2 = self.pb[0], self.pb[1], self.pb[2]
                S.op("act", lambda e, sl=sl, p0=p0: e.activation(out=qT[:, sl], in_=p0[:], func=AF.Copy, scale=0.125),
                     reads=["pb0"], writes=[f"q{t}"])
                S.op("dve", lambda e, sl=sl, p1=p1: e.tensor_copy(out=kT[0:64, sl], in_=p1[0:64, :]),
                     reads=["pb1"], writes=[f"k{t}"])
                S.op("dve", lambda e, sl=sl, p1=p1: e.tensor_copy(out=kTB[64:128, sl], in_=p1[64:128, :]),
                     reads=["pb1"], writes=[f"kb{t}"])
                S.op("dve", lambda e, sl=sl, p2=p2: e.tensor_copy(out=vT[:, sl], in_=p2[:]),
                     reads=["pb2"], writes=[f"v{t}"])

            allv = [f"v{t}" for t in range(NT)]
            ptbufs = [(self.pt[:], "pt"), (self.pt[:], "pt")]

            def emit_tr(lay, half, bufi):
                ptb, ptn = ptbufs[bufi]
                for bi in range(8):
                    blk = half * 8 + bi
                    if lay == 0:
                        src = vT[:, blk * 128:(blk + 1) * 128]
                    elif lay == 1:
                        src = v4[:, blk // 4, (blk % 4) * 128:(blk % 4 + 1) * 128]
                    else:
                        src = v16[:, blk, :]
                    dst = ptb[:, bi * 128:(bi + 1) * 128]
                    S.op("pe", lambda e, dst=dst, src=src: e.transpose(out=dst, in_=src, identity=self.ident[:]),
                         reads=allv + ["ident"], writes=[ptn])
                b0 = lay * 16 + half * 8
                dst = bass.AP(R, R_VT + b0 * 192, [[R_N, 128], [192, 8], [128, 2], [1, 64]])
                src = ptb.rearrange("p (b h c) -> p b h c", b=8, h=2)
                S.op("dve", lambda e, dst=dst, src=src: e.tensor_copy(out=dst, in_=src),
                     reads=[ptn], writes=[f"V{lay}_{half}"])

            tr_groups = [(0, 0), (1, 0), (1, 1), (2, 0), (2, 1), (0, 1)]
            emit_tr(0, 0, 0)

            units = []
            for T in range(NT):
                for hd in range(2):
                    rows = slice(0, 128)
                    kT_ = kTs[hd]
                    k4 = k4s[hd]
                    k16 = k16s[hd]
                    kn = "k" if hd == 0 else "kb"
                    kz = "kzA" if hd == 0 else "kzB"
                    vc = slice(0, 128) if hd == 0 else slice(64, 192)
                    obank = 3 + 2 * hd
                    obank2 = 4 + 2 * hd
                    O = self.pb[obank]
                    O2 = self.pb[obank2]
                    O16 = O[:].rearrange("p (m r) -> p r m", r=16)
                    ulist = []
                    for uu in range(2):
                        qk = []
                        pv = []
                        for qq in range(2):
                            m = 4 * T + 2 * uu + qq
                            off = qq * 256
                            qs = qT[rows, m * 128:(m + 1) * 128]
                            qk.append((off, 128, kT_[rows, m * 128:(m + 1) * 128], qs, [f"q{m // 4}", f"{kn}{m // 4}", kz]))
                            oc = O[:, (m - 4 * T) * 128:(m - 4 * T + 1) * 128]
                            pv.append((oc, Vt[:, m, vc], off, 128, [f"V0_{m // 8}"]))
                            if m > 0:
                                qk.append((off + 128, 128, kT_[rows, (m - 1) * 128:m * 128], qs,
                                           [f"q{m // 4}", f"{kn}{(m - 1) // 4}"]))
                                pv.append((oc, Vt[:, m - 1, vc], off + 128, 128, [f"V0_{(m - 1) // 8}"]))
                        Eap = Et[:, hd, 0:256].unsqueeze(1).to_broadcast([128, 2, 256])
                        ulist.append(dict(qk=qk, pv=pv, E=Eap, kk=128, eshape=(2, 256)))
                    for uu in range(2):
                        qk = []
                        pv = []
                        for qq in range(2):
                            r = 2 * uu + qq
                            off = qq * 256
                            qs = q4[rows, r, T * 128:(T + 1) * 128]
                            qk.append((off, 128, k4[rows, r, T * 128:(T + 1) * 128], qs, [f"q{T}", f"{kn}{T}"]))
                            oc = O2[:, r * 128:(r + 1) * 128]
                            blk = 16 + r * 4 + T
                            pv.append((oc, Vt[:, blk, vc], off, 128, [f"V1_{(blk - 16) // 8}"]))
                            if T > 0:
                                qk.append((off + 128, 128, k4[rows, r, (T - 1) * 128:T * 128], qs, [f"q{T}", f"{kn}{T - 1}"]))
                                pv.append((oc, Vt[:, blk - 1, vc], off + 128, 128, [f"V1_{(blk - 17) // 8}"]))
                        Eap = Et[:, hd, 256:512].unsqueeze(1).to_broadcast([128, 2, 256])
                        ulist.append(dict(qk=qk, pv=pv, E=Eap, kk=128, eshape=(2, 256), br2=True))
                    kk = 128
                    krows = slice(0, 128)
                    qk = []
                    pv = []
                    for r in range(16):
                        qk.append((r * 32, 32, k16[krows, r, 0:kk], q16[rows, r, 32 * T:32 * T + 32],
                                   [f"q{T}"] + [f"{kn}{tt}" for tt in range(NT)]))
                        pv.append((O16[:, r, :], Vt[0:kk, 32 + r, vc], r * 32, 32, [f"V2_{r // 8}"]))
                    Eap = Et[0:kk, hd, 512 + 32 * T:512 + 32 * T + 32].unsqueeze(1).to_broadcast([kk, 16, 32])
                    ulist.append(dict(qk=qk, pv=pv, E=Eap, kk=kk, eshape=(16, 32)))
                    for ui, u in enumerate(ulist):
                        u["T"] = T
                        u["hd"] = hd
                        is2 = u.get("br2", False)
                        u["obank"] = obank2 if is2 else obank
                        u["obank1"] = obank
                        u["obank2"] = obank2
                        u["first"] = (ui == 0) or (ui == 2)
                        u["last"] = (ui == len(ulist) - 1)
                        u["last2"] = (ui == 3)
                        units.append(u)

            def emit_qk(u, ui):
                sb = ui % 3
                u["sb"] = sb
                Sps = self.pb[sb]
                kk = u["kk"]
                for (off, n, lh, rh, rd) in u["qk"]:
                    self.mm(Sps[0:kk, off:off + n], lh, rh, True, True, reads=rd, writes=[f"pb{sb}"])
                pi = ui % NPP
                u["pi"] = pi
                p1i = ui % NP1
                p1 = P1[p1i]
                S.op("act", lambda e, p1=p1, Sps=Sps, kk=kk: e.activation(out=p1[0:kk, :], in_=Sps[0:kk, :], func=AF.Exp),
                     reads=[f"pb{sb}"], writes=[f"P1_{p1i}"])
                a, b = u["eshape"]
                pp = PP[pi]
                ppv = pp[0:kk, :].rearrange("p (a b) -> p a b", a=a)
                p1v = p1[0:kk, :].rearrange("p (a b) -> p a b", a=a)
                Eap = u["E"]
                eng = "pool" if ui % 2 == 0 else "dve"
                S.op(eng, lambda e, ppv=ppv, p1v=p1v, Eap=Eap: e.tensor_tensor(out=ppv, in0=p1v, in1=Eap, op=ALU.mult),
                     reads=[f"P1_{p1i}", "E"], writes=[f"P_{pi}"])

            def emit_pv(u):
                ob = u["obank"]
                pp = PP[u["pi"]]
                kk = u["kk"]
                npv = len(u["pv"])
                for i, (oc, lh, off, n, rd) in enumerate(u["pv"]):
                    st = u["first"] and i == 0
                    sp_ = (u["last"] or u["last2"]) and i == npv - 1
                    self.mm(oc, lh, pp[0:kk, off:off + n], st, sp_, reads=rd + [f"P_{u['pi']}", "Vones"],
                            writes=[f"pb{ob}"], skip=True)
                T, hd = u["T"], u["hd"]
                ev = self.tmp[(T % 2) * 2 + hd]
                evn = f"tmp{(T % 2) * 2 + hd}"
                if u["last2"]:
                    O2 = self.pb[u["obank2"]]
                    evv = ev[:].rearrange("p (i r) -> p r i", r=4)
                    o2v = O2[:].rearrange("p (r i) -> p r i", r=4)
                    S.op("act", lambda e, evv=evv, o2v=o2v: e.activation(out=evv, in_=o2v, func=AF.Copy),
                         reads=[f"pb{u['obank2']}"], writes=[evn])
                if u["last"]:
                    ob1 = u["obank1"]
                    O = self.pb[ob1]
                    S.op("dve", lambda e, ev=ev, O=O: e.tensor_tensor(out=ev[:], in0=O[:], in1=ev[:], op=ALU.add),
                         reads=[f"pb{ob1}", evn], writes=[evn])
                    rc = self.rs[hd]
                    rn = f"rs{hd}"
                    if hd == 0:
                        den, num, orow = ev[64:128, :], ev[0:64, :], slice(0, 64)
                    else:
                        den, num, orow = ev[0:64, :], ev[64:128, :], slice(64, 128)
                    rcv = rc[orow, :]
                    S.op("act", lambda e, rcv=rcv, den=den: e.activation(out=rcv, in_=den, func=AF.Ln),
                         reads=[evn], writes=[rn])
                    S.op("act", lambda e, rcv=rcv: e.activation(out=rcv, in_=rcv, func=AF.Exp, scale=-1.0),
                         reads=[rn], writes=[rn])
                    oo = self.os_(hp, T)[orow, :]
                    S.op("dve", lambda e, oo=oo, num=num, rcv=rcv: e.tensor_tensor(out=oo, in0=num, in1=rcv, op=ALU.mult),
                         reads=[evn, rn], writes=[f"o{hp}_{T}"])

            SKEW = 4
            nu = len(units)
            for i in range(nu + SKEW):
                if i < nu:
                    emit_qk(units[i], i)
                if i < 5:
                    emit_tr(tr_groups[i + 1][0], tr_groups[i + 1][1], (i + 1) % 2)
                if i >= SKEW:
                    emit_pv(units[i - SKEW])

    def build(self):
        self.plan_weights()
        self.prologue()
        self._issue_w(0)
        self.wissued = 1
        first = True
        for l in self.layers:
            if l % 2 == 1:
                self.run_side(1000)
                self.attn_prep()
            self.norm(2 * l)
            if first and any(ll % 2 == 1 for ll in self.layers):
                self.build_tables()
                if l % 2 == 1:
                    self.run_side(1000)
            first = False
            if l % 2 == 0:
                self.conv_mixer(l // 2)
            else:
                self.attn_mixer()
            self.out_proj()
            self.norm(2 * l + 1)
            self.ffn()
        if self.final:
            self.norm(8, final=True)
        S = self.S
        keys = self.out_keys
        if not self.final:
            for c in range(KC):
                src = self.xT[:, c * SEQ:(c + 1) * SEQ]
                dst = self.yd[:, c * SEQ:(c + 1) * SEQ]
                S.dma("sp", lambda e, dst=dst, src=src: e.dma_start(out=dst, in_=src),
                      reads=[f"x{c}_{t}" for t in range(NT)], sem_key=f"yo{c}")
                keys.append(f"yo{c}")
        S.emit(self.nc, final_wait_keys=keys)
        return self.nc


def _t5_bucket(dist):
    exact = 16
    df = np.maximum(dist, 1).astype(np.float32)
    large = exact + (np.log(df / np.float32(exact)) / np.float32(math.log(2048 / exact))
                     * np.float32(32 - exact)).astype(np.int32)
    large = np.minimum(large, 31)
    return np.where(dist < exact, dist, large)


def _const_tables():
    oh = np.zeros((33, 3, GL), np.float32)
    for br, dil in enumerate((1, 4, 16)):
        for u in range(GL):
            rel = u - 127
            if 0 <= rel <= 128:
                b = int(_t5_bucket(np.array([rel * dil], np.int32))[0])
                oh[b, br, u] = 1.0
            else:
                oh[32, br, u] = -30000.0
    return oh.reshape(33, 3 * GL)


def _kmaj(w):
    K = w.shape[0] // 128
    return w.reshape(K, 128, w.shape[1]).transpose(1, 0, 2)


def prep_shared(inp, layers=(0, 1, 2, 3)):
    m = {}
    g = np.concatenate([np.stack([inp["mix_norm"][l], inp["ffn_norm"][l]]) for l in range(NL)]
                       + [inp["final_norm"][None]], axis=0)
    m["gains"] = np.ascontiguousarray(g.reshape(9, 8, 128).transpose(2, 0, 1).reshape(128, 72))
    ck = inp["conv_kernel"]
    m["ck"] = np.ascontiguousarray(ck.reshape(2, 3, 8, 128).transpose(3, 0, 2, 1).reshape(128, 48))
    m["ident"] = np.eye(128, dtype=np.float32)
    m["oh"] = _const_tables()
    m["rbx"] = np.concatenate([inp["rel_bias"], np.ones((1, 16), np.float32)], axis=0)
    for l in layers:
        j = l // 2
        wi = inp["conv_w_in"][j] if l % 2 == 0 else inp["attn_w_qkv"][j]
        wo = inp["conv_w_out"][j] if l % 2 == 0 else inp["attn_w_out"][j]
        a = _kmaj(wi).reshape(128, 8, 3, 8, 128)
        m[f"wmix{l}"] = np.ascontiguousarray(a.transpose(3, 0, 1, 2, 4).reshape(8, 128, 3072))
        a = _kmaj(wo).reshape(128, 8, 4, 2, 128)
        m[f"wout{l}"] = np.ascontiguousarray(a.transpose(2, 0, 3, 1, 4).reshape(4, 128, 2048))
        ag = _kmaj(inp["ffn_w_gate"][l]).reshape(128, 8, 22, 128)
        au = _kmaj(inp["ffn_w_up"][l]).reshape(128, 8, 22, 128)
        a = np.stack([ag, au], axis=2)
        m[f"wgu{l}"] = np.ascontiguousarray(a.transpose(3, 0, 1, 2, 4).reshape(22, 128, 2048))
        a = _kmaj(inp["ffn_w_down"][l]).reshape(128, 2, FG, 4, 2, 128)
        m[f"wdn{l}"] = np.ascontiguousarray(a.transpose(1, 3, 0, 4, 2, 5).reshape(8, 128, 2816))
    return m


def x_to_dev(xb):
    return np.ascontiguousarray(xb.T.reshape(8, 128, SEQ).transpose(1, 0, 2).reshape(128, 8 * SEQ))


def y_from_dev(y):
    return np.ascontiguousarray(y.reshape(128, 8, SEQ).transpose(1, 0, 2).reshape(D, SEQ).T)


def kernel(**inputs):
    inp = {k: np.asarray(v, dtype=np.float32) for k, v in inputs.items()}
    shared = prep_shared(inp)
    nc = Builder().build()
    nb = inp["x"].shape[0]
    in_maps = []
    for b in range(nb):
        m = dict(shared)
        m["x"] = x_to_dev(inp["x"][b])
        in_maps.append(m)
    res = run_bass_kernel_spmd(nc, in_maps, core_ids=list(range(nb)))
    out = np.stack([y_from_dev(np.asarray(res.results[b]["y"], dtype=np.float32)) for b in range(nb)], axis=0)
    return out.astype(np.float32)
```

```python
import math
import numpy as np
import concourse.bass as bass
import concourse.mybir as mybir
from concourse.bass_utils import run_bass_kernel_spmd

F32 = mybir.dt.float32
BF16 = mybir.dt.bfloat16
AF = mybir.ActivationFunctionType
ALU = mybir.AluOpType

D = 1024
SEQ = 2048
DFF = 2816
NL = 4
NT = 4
TW = 512
KC = 8
FG = 11
GL = 384
WSLOT = 3072
NSLOT = 3

ENGS = ("pe", "act", "dve", "pool", "sp")
SEM_K = 6
SEM_CH = 512


class Op:
    __slots__ = ("eng", "fn", "idx", "rdeps", "wdeps", "is_dma", "sem_key", "val",
                 "needs_signal", "sig", "waits", "nstarts")


class Sched:
    def __init__(self):
        self.ops = {e: [] for e in ENGS}
        self.lastw = {}
        self.readers = {}
        self.dma_count = {}

    def _add(self, eng, fn, reads, writes, is_dma=False, sem_key=None, nstarts=1):
        o = Op()
        o.eng = eng
        o.fn = fn
        o.idx = len(self.ops[eng])
        o.is_dma = is_dma
        o.sem_key = sem_key
        o.needs_signal = False
        o.sig = -1
        o.nstarts = nstarts
        o.val = 0
        if is_dma:
            c = self.dma_count.get(sem_key, 0) + nstarts
            self.dma_count[sem_key] = c
            o.val = 16 * c
        rdeps = []
        wdeps = []
        for r in reads:
            w = self.lastw.get(r)
            if w is not None:
                rdeps.append(w)
        for r in writes:
            w = self.lastw.get(r)
            if w is not None:
                wdeps.append(w)
            rd = self.readers.get(r)
            if rd:
                for x in rd.values():
                    if isinstance(x, list):
                        wdeps.extend(x)
                    else:
                        wdeps.append(x)
        for r in reads:
            d = self.readers.setdefault(r, {})
            if is_dma:
                d.setdefault("dma", []).append(o)
            else:
                d[eng] = o
        for r in writes:
            self.lastw[r] = o
            self.readers[r] = {}
        o.rdeps = rdeps
        o.wdeps = wdeps
        self.ops[eng].append(o)
        return o

    def op(self, eng, fn, reads=(), writes=()):
        return self._add(eng, fn, reads, writes)

    def dma(self, queue, fn, reads=(), writes=(), sem_key=None, nstarts=1):
        return self._add(queue, fn, reads, writes, True, sem_key, nstarts)

    def resolve(self):
        for e in ENGS:
            seen_idx = {x: -1 for x in ENGS}
            seen_dma = {}
            for o in self.ops[e]:
                best = {}
                final = []
                deps = [(d, True) for d in o.rdeps] + [(d, False) for d in o.wdeps]
                for d, is_raw in deps:
                    if d is o:
                        continue
                    if d.is_dma:
                        if seen_dma.get(d.sem_key, 0) >= d.val:
                            continue
                        seen_dma[d.sem_key] = d.val
                        final.append(d)
                        continue
                    if d.eng == e:
                        if e == "pe":
                            continue
                        if not (is_raw or o.is_dma):
                            continue
                    if seen_idx[d.eng] >= d.idx:
                        continue
                    seen_idx[d.eng] = d.idx
                    b = best.get(d.eng)
                    if b is None or d.idx > b.idx:
                        best[d.eng] = d
                for d in best.values():
                    d.needs_signal = True
                    final.append(d)
                o.waits = final
        for e in ENGS:
            n = 0
            for o in self.ops[e]:
                if o.needs_signal and not o.is_dma:
                    o.sig = n
                    n += 1

    @staticmethod
    def sem_slot_val(sig):
        c = sig // SEM_CH
        slot = c % SEM_K
        val = (c // SEM_K) * SEM_CH + (sig % SEM_CH) + 1
        return slot, val

    def emit(self, nc, final_wait_keys=()):
        self.resolve()
        import contextlib
        with contextlib.ExitStack() as st:
            esems = {}
            for e in ENGS:
                esems[e] = [st.enter_context(nc.semaphore(f"s_{e}_{i}")) for i in range(SEM_K)]
            dsems = {}
            for k in self.dma_count:
                dsems[k] = st.enter_context(nc.semaphore(f"d_{k}"))
            block = st.enter_context(nc.Block())

            def run(e, eng):
                for o in self.ops[e]:
                    for d in o.waits:
                        if d.is_dma:
                            eng.wait_ge(dsems[d.sem_key], d.val)
                        else:
                            slot, val = self.sem_slot_val(d.sig)
                            eng.wait_ge(esems[d.eng][slot], val)
                    r = o.fn(eng)
                    if o.is_dma:
                        if not isinstance(r, (list, tuple)):
                            r = [r]
                        assert len(r) == o.nstarts, (len(r), o.nstarts)
                        for ins in r:
                            ins.then_inc(dsems[o.sem_key], 16)
                    elif o.needs_signal:
                        slot, _ = self.sem_slot_val(o.sig)
                        r.then_inc(esems[e][slot], 1)
                if e == "sp":
                    for k in final_wait_keys:
                        eng.wait_ge(dsems[k], 16 * self.dma_count[k])

            @block.tensor
            def _(eng):
                run("pe", eng)

            @block.scalar
            def _(eng):
                run("act", eng)

            @block.vector
            def _(eng):
                run("dve", eng)

            @block.gpsimd
            def _(eng):
                run("pool", eng)

            @block.sync
            def _(eng):
                run("sp", eng)


R_O = 0
R_Q = 16384
R_K = 18432
R_KB = 20480
R_V = 22528
R_VT = 24576
R_E = 33792
NP1 = 3
NPP = 5
R_P1 = 35072
R_P = R_P1 + NP1 * 512
R_N = R_P + NPP * 512


class Builder:
    def __init__(self, layers=(0, 1, 2, 3), final=True):
        self.layers = tuple(layers)
        self.final = final
        nc = self.nc = bass.Bass("TRN2", target_bir_lowering=False)
        self.S = Sched()
        dt = nc.dram_tensor
        self.xd = dt("x", [128, KC * SEQ], F32, kind="ExternalInput").ap()
        self.gd = dt("gains", [128, 72], F32, kind="ExternalInput").ap()
        self.ckd = dt("ck", [128, 48], F32, kind="ExternalInput").ap()
        self.idd = dt("ident", [128, 128], F32, kind="ExternalInput").ap()
        self.ohd = dt("oh", [33, 3 * GL], F32, kind="ExternalInput").ap()
        self.rbd = dt("rbx", [33, 16], F32, kind="ExternalInput").ap()
        self.wmix = {}
        self.wout = {}
        self.wgu = {}
        self.wdn = {}
        for l in self.layers:
            self.wmix[l] = dt(f"wmix{l}", [8, 128, 3072], F32, kind="ExternalInput").ap()
            self.wout[l] = dt(f"wout{l}", [4, 128, 2048], F32, kind="ExternalInput").ap()
            self.wgu[l] = dt(f"wgu{l}", [22, 128, 2048], F32, kind="ExternalInput").ap()
            self.wdn[l] = dt(f"wdn{l}", [8, 128, 2816], F32, kind="ExternalInput").ap()
        self.yd = dt("y", [128, KC * SEQ], F32, kind="ExternalOutput").ap()
        self.gscr = dt("gscr", [128, 48 * GL], BF16).ap()

        a = nc.alloc_sbuf_tensor
        self.xT = a("xT", [128, KC * SEQ], F32)
        self.hT = a("hT", [128, KC * SEQ], BF16)
        self.R = a("R", [128, R_N], BF16)
        self.wsl = [a(f"wsl{i}", [128, WSLOT], BF16) for i in range(NSLOT)]
        self.rs = [a(f"rs{i}", [128, TW], F32) for i in range(2)]
        self.tmp = [a(f"tmp{i}", [128, TW], F32) for i in range(4)]
        self.sq = [a(f"sq{i}", [128, TW], BF16) for i in range(2)]
        self.gains = a("gains_sb", [128, 72], F32)
        self.bar = a("bar_sb", [128, 20], F32)
        self.tab = a("tab_sb", [128, 1152], BF16)
        self.ck = a("ck_sb", [128, 48], F32)
        self.ident = a("ident_sb", [128, 128], BF16)
        self.ones = a("ones_sb", [128, 128], BF16)
        self.pb = [nc.alloc_psum_tensor(f"pb{i}", [128, TW], F32) for i in range(7)]
        self.pt = nc.alloc_psum_tensor("pt", [128, 1024], BF16)

        print("sbuf remaining", nc.sbuf_bytes_remaining)
        self.out_keys = []
        self.side_jobs = []
        self.wlist = []
        self.wissued = 0
        self.wptr = 0
        self.cnt = 0

    def xs(self, c, t):
        return self.xT[:, c * SEQ + t * TW: c * SEQ + (t + 1) * TW]

    def hs(self, c, t):
        return self.hT[:, c * SEQ + t * TW: c * SEQ + (t + 1) * TW]

    def os_(self, c, t):
        return self.R[:, R_O + c * SEQ + t * TW: R_O + c * SEQ + (t + 1) * TW]

    def mm(self, out, lhsT, rhs, start, stop, reads, writes, skip=False):
        if skip:
            fn = lambda e: e.matmul(out, lhsT=lhsT, rhs=rhs, start=start, stop=stop, skip_group_check=True)
        else:
            fn = lambda e: e.matmul(out, lhsT=lhsT, rhs=rhs, start=start, stop=stop)
        return self.S.op("pe", fn, reads=reads, writes=writes)

    def plan_weights(self):
        for l in self.layers:
            for j in range(8):
                self.wlist.append((self.wmix[l][j], 3072))
            for half in range(2):
                for i in range(4):
                    self.wlist.append((self.wout[l][i], 2048))
            for g in range(2):
                for fl in range(FG):
                    self.wlist.append((self.wgu[l][g * FG + fl], 2048))
                for half in range(2 if g == 1 else 1):
                    for i in range(4):
                        self.wlist.append((self.wdn[l][g * 4 + i], 2816))

    def _issue_w(self, i):
        src, L = self.wlist[i]
        slot = i % NSLOT
        dst = self.wsl[slot][:, 0:L]
        self.S.dma("pool", lambda e: e.dma_start(out=dst, in_=src), writes=[f"w{slot}"], sem_key=f"w{slot}")

    def acquire(self):
        i = self.wptr
        while self.wissued < min(len(self.wlist), i + NSLOT):
            self._issue_w(self.wissued)
            self.wissued += 1
        self.wptr += 1
        slot = i % NSLOT
        return self.wsl[slot], f"w{slot}"

    def prologue(self):
        S = self.S
        S.dma("sp", lambda e: e.dma_start(out=self.gains[:], in_=self.gd), writes=["gains"], sem_key="misc")
        S.dma("sp", lambda e: e.dma_start(out=self.ck[:], in_=self.ckd), writes=["ck"], sem_key="misc2")
        S.dma("pool", lambda e: e.dma_start(out=self.ident[:], in_=self.idd), writes=["ident"], sem_key="ident")
        S.op("dve", lambda e: e.memset(self.ones[:], 1.0), writes=["ones"])
        xv = self.xT[:].rearrange("p (c t w) -> p c t w", c=KC, t=NT)
        xdv = self.xd.rearrange("p (c t w) -> p c t w", c=KC, t=NT)
        for t in range(NT):
            dst = xv[:, :, t, :]
            src = xdv[:, :, t, :]
            S.dma("sp", lambda e, dst=dst, src=src: e.dma_start(out=dst, in_=src),
                  writes=[f"x{c}_{t}" for c in range(KC)], sem_key=f"xl{t}")

    def build_tables(self):
        S = self.S
        G = self.R[:, R_Q:R_Q + 48 * GL]
        RBs = [self.rs[0][:].bitcast(BF16), self.rs[1][:].bitcast(BF16)]
        RBX = self.bar[:, 4:20]
        OH = self.tab[:, 0:1152]
        S.dma("pool", lambda e: e.dma_start(out=OH[0:33, :], in_=self.ohd), writes=["OH"], sem_key="oh")
        S.dma("sp", lambda e: e.dma_start(out=RBX[0:33, :], in_=self.rbd), writes=["RBX"], sem_key="rbx")
        for h in range(16):
            dst = RBs[h // 8][0:33, (h % 8) * 128:(h % 8 + 1) * 128]
            src = RBX[0:33, h:h + 1].to_broadcast([33, 128])
            S.op("pool", lambda e, dst=dst, src=src: e.tensor_copy(out=dst, in_=src),
                 reads=["RBX"], writes=[f"rs{h // 8}"])
        def job(h, br, n):
            def run():
                ps = self.pt[:].bitcast(F32)[:, 0:GL]
                self.mm(ps, RBs[h // 8][0:33, (h % 8) * 128:(h % 8 + 1) * 128], OH[0:33, br * GL:(br + 1) * GL],
                        True, True, reads=[f"rs{h // 8}", "OH"], writes=["pt"])
                dst = G[:, n * GL:(n + 1) * GL]
                S.op("act", lambda e, dst=dst, ps=ps: e.activation(out=dst, in_=ps, func=AF.Exp),
                     reads=["pt"], writes=["G"])
            return run

        def finish():
            self._tables_finish(G)

        jobs = []
        n = 0
        for h in range(16):
            for br in range(3):
                jobs.append(job(h, br, n))
                n += 1
        jobs.append(finish)
        self.side_jobs = jobs

    def run_side(self, k):
        for _ in range(k):
            if self.side_jobs:
                self.side_jobs.pop(0)()

    def _tables_finish(self, G):
        S = self.S
        S.dma("sp", lambda e: e.dma_start(out=self.gscr, in_=G), reads=["G"], writes=["gscr"], sem_key="gscr")
        names = ["G", "Vones", "E", "kzA", "kzB"]
        names += [f"o{c}_{t}" for c in range(8, FG) for t in range(NT)]
        for t in range(NT):
            names += [f"q{t}", f"k{t}", f"kb{t}", f"v{t}"]
        names += [f"V{lay}_{half}" for lay in range(3) for half in range(2)]
        names += [f"P1_{i}" for i in range(NP1)] + [f"P_{i}" for i in range(NPP)]
        S.dma("sp", lambda e: e.dma_start(out=self.bar[0:1, 0:4], in_=self.gd[0:1, 0:4]), writes=names, sem_key="bar")

    def norm(self, gi, final=False):
        S = self.S
        for t in range(NT):
            ss = self.pb[6]
            for c in range(KC):
                sq = self.sq[c % 2]
                xin = self.xs(c, t)
                S.op("act", lambda e, sq=sq, xin=xin: e.activation(out=sq[:], in_=xin, func=AF.Square),
                     reads=[f"x{c}_{t}"], writes=[f"sq{c % 2}"])
                self.mm(ss[:], self.ones[:], sq[:], c == 0, c == KC - 1, reads=[f"sq{c % 2}", "ones"], writes=["pb6"])
            rs = self.rs[t % 2]
            S.op("dve", lambda e, rs=rs, ss=ss: e.tensor_scalar(out=rs[:], in0=ss[:], scalar1=1.0 / D, scalar2=1e-6,
                                                              op0=ALU.mult, op1=ALU.add),
                 reads=["pb6"], writes=[f"rs{t % 2}"])
            S.op("act", lambda e, rs=rs: e.activation(out=rs[:], in_=rs[:], func=AF.Ln),
                 reads=[f"rs{t % 2}"], writes=[f"rs{t % 2}"])
            S.op("act", lambda e, rs=rs: e.activation(out=rs[:], in_=rs[:], func=AF.Exp, scale=-0.5),
                 reads=[f"rs{t % 2}"], writes=[f"rs{t % 2}"])
            for c in range(KC):
                xin = self.xs(c, t)
                g = self.gains[:, gi * 8 + c: gi * 8 + c + 1]
                if final:
                    S.op("dve", lambda e, xin=xin, g=g, rs=rs: e.scalar_tensor_tensor(
                        out=xin, in0=xin, scalar=g, in1=rs[:], op0=ALU.mult, op1=ALU.mult),
                        reads=[f"x{c}_{t}", f"rs{t % 2}", "gains"], writes=[f"x{c}_{t}"])
                    if c == KC - 1:
                        xv = self.xT[:].rearrange("p (c t w) -> p c t w", c=KC, t=NT)[:, :, t, :]
                        yv = self.yd.rearrange("p (c t w) -> p c t w", c=KC, t=NT)[:, :, t, :]
                        S.dma("sp", lambda e, xv=xv, yv=yv: e.dma_start(out=yv, in_=xv),
                              reads=[f"x{cc}_{t}" for cc in range(KC)], sem_key=f"yo{t}")
                        self.out_keys.append(f"yo{t}")
                else:
                    ho = self.hs(c, t)
                    if True:
                        S.op("dve", lambda e, xin=xin, g=g, rs=rs, ho=ho: e.scalar_tensor_tensor(
                            out=ho, in0=xin, scalar=g, in1=rs[:], op0=ALU.mult, op1=ALU.mult),
                            reads=[f"x{c}_{t}", f"rs{t % 2}", "gains"], writes=[f"h{c}_{t}"])
                    else:
                        pt_ = self.ptmp
                        S.op("pool", lambda e, xin=xin, g=g, pt_=pt_: e.tensor_scalar(
                            out=pt_[:], in0=xin, scalar1=g, scalar2=None, op0=ALU.mult),
                            reads=[f"x{c}_{t}", "gains"], writes=["ptmp"])
                        S.op("pool", lambda e, rs=rs, ho=ho, pt_=pt_: e.tensor_tensor(
                            out=ho, in0=pt_[:], in1=rs[:], op=ALU.mult),
                            reads=["ptmp", f"rs{t % 2}"], writes=[f"h{c}_{t}"])

    def out_proj(self):
        S = self.S
        for half, i in [(h_, i_) for h_ in range(2) for i_ in range(4)]:
            w, wr = self.acquire()
            wv = w[:, 0:2048].rearrange("p (nn k n) -> p nn k n", nn=2, k=KC)
            for nn in range(2):
                n = 2 * i + nn
                for t in range(2 * half, 2 * half + 2):
                    b = self.cnt % 5
                    self.cnt += 1
                    ps = self.pb[b]
                    for k in range(KC):
                        self.mm(ps[:], wv[:, nn, k, :], self.os_(k, t), k == 0, k == KC - 1,
                                reads=[wr, f"o{k}_{t}"], writes=[f"pb{b}"])
                    xo = self.xs(n, t)
                    S.op("dve", lambda e, xo=xo, ps=ps: e.tensor_tensor(out=xo, in0=ps[:], in1=xo, op=ALU.add),
                         reads=[f"pb{b}", f"x{n}_{t}"], writes=[f"x{n}_{t}"])

    def ffn(self):
        S = self.S
        for g in range(2):
            for fl in range(FG):
                w, wr = self.acquire()
                wv = w[:, 0:2048].rearrange("p (k r n) -> p k r n", k=KC, r=2)
                for t in range(NT):
                    pr = self.cnt % 2
                    self.cnt += 1
                    bG, bU = 2 * pr, 2 * pr + 1
                    for k in range(KC):
                        self.mm(self.pb[bG][:], wv[:, k, 0, :], self.hs(k, t), k == 0, k == KC - 1,
                                reads=[wr, f"h{k}_{t}"], writes=[f"pb{bG}"])
                    for k in range(KC):
                        self.mm(self.pb[bU][:], wv[:, k, 1, :], self.hs(k, t), k == 0, k == KC - 1,
                                reads=[wr, f"h{k}_{t}"], writes=[f"pb{bU}"])
                    tm = self.tmp[pr]
                    pG = self.pb[bG]
                    pU = self.pb[bU]
                    S.op("act", lambda e, tm=tm, pG=pG: e.activation(out=tm[:], in_=pG[:], func=AF.Silu),
                         reads=[f"pb{bG}"], writes=[f"tmp{pr}"])
                    ao = self.os_(fl, t) if True else None
                    S.op("dve", lambda e, ao=ao, pU=pU, tm=tm: e.tensor_tensor(out=ao, in0=pU[:], in1=tm[:], op=ALU.mult),
                         reads=[f"pb{bU}", f"tmp{pr}"], writes=[f"o{fl}_{t}"])
            passes = [(None, i_) for i_ in range(4)] if g == 0 else [(h_, i_) for h_ in range(2) for i_ in range(4)]
            for half, i in passes:
                w, wr = self.acquire()
                wv = w[:, 0:2816].rearrange("p (nn f n) -> p nn f n", nn=2, f=FG)
                for nn in range(2):
                    n = 2 * i + nn
                    for t in (range(NT) if half is None else range(2 * half, 2 * half + 2)):
                        b = 4 + self.cnt % 2
                        self.cnt += 1
                        ps = self.pb[b]
                        for fl in range(FG):
                            self.mm(ps[:], wv[:, nn, fl, :], self.os_(fl, t), fl == 0, fl == FG - 1,
                                    reads=[wr, f"o{fl}_{t}"], writes=[f"pb{b}"])
                        xo = self.xs(n, t)
                        S.op("dve", lambda e, xo=xo, ps=ps: e.tensor_tensor(out=xo, in0=ps[:], in1=xo, op=ALU.add),
                             reads=[f"pb{b}", f"x{n}_{t}"], writes=[f"x{n}_{t}"])

    def conv_mixer(self, l2):
        S = self.S
        cu = self.R[:, R_N - 4224:R_N].bitcast(F32)
        S.op("dve", lambda e: e.memset(cu[:, 0:2], 0.0), writes=["cuh"])
        for j in range(8):
            w, wr = self.acquire()
            wv = w[:, 0:3072].rearrange("p (k r n) -> p k r n", k=KC, r=3)
            for t in range(NT):
                pr = self.cnt % 2
                self.cnt += 1
                bC, bU = 2 * pr, 2 * pr + 1
                bB = 4 + (self.cnt // 1) % 3
                for r, b in ((2, bU), (1, bC), (0, bB)):
                    for k in range(KC):
                        self.mm(self.pb[b][:], wv[:, k, r, :], self.hs(k, t), k == 0, k == KC - 1,
                                reads=[wr, f"h{k}_{t}"], writes=[f"pb{b}"])
                self.run_side(2)
                tu = self.tmp[pr]
                ty = self.tmp[2 + pr]
                pB, pC, pU = self.pb[bB], self.pb[bC], self.pb[bU]
                S.op("act", lambda e, tu=tu, pU=pU: e.activation(out=tu[:], in_=pU[:], func=AF.Copy),
                     reads=[f"pb{bU}"], writes=[f"tmp{pr}"])
                c0 = cu[:, 2 + t * TW: 2 + (t + 1) * TW]
                c1 = cu[:, 1 + t * TW: 1 + (t + 1) * TW]
                c2 = cu[:, t * TW: (t + 1) * TW]
                S.op("dve", lambda e, c0=c0, pC=pC, tu=tu: e.tensor_tensor(out=c0, in0=pC[:], in1=tu[:], op=ALU.mult),
                     reads=[f"pb{bC}", f"tmp{pr}"], writes=[f"cu{t}"])
                kb = (l2 * 8 + j) * 3
                k0 = self.ck[:, kb:kb + 1]
                k1 = self.ck[:, kb + 1:kb + 2]
                k2 = self.ck[:, kb + 2:kb + 3]
                prev = [f"cu{t - 1}"] if t > 0 else ["cuh"]
                S.op("act", lambda e, ty=ty, c0=c0, k2=k2: e.activation(out=ty[:], in_=c0, func=AF.Copy, scale=k2),
                     reads=[f"cu{t}", "ck"], writes=[f"tmp{2 + pr}"])
                S.op("dve", lambda e, ty=ty, c1=c1, k1=k1: e.scalar_tensor_tensor(out=ty[:], in0=c1, scalar=k1, in1=ty[:],
                                                                                op0=ALU.mult, op1=ALU.add),
                     reads=[f"cu{t}", f"tmp{2 + pr}"] + prev, writes=[f"tmp{2 + pr}"])
                S.op("dve", lambda e, ty=ty, c2=c2, k0=k0: e.scalar_tensor_tensor(out=ty[:], in0=c2, scalar=k0, in1=ty[:],
                                                                                op0=ALU.mult, op1=ALU.add),
                     reads=[f"cu{t}", f"tmp{2 + pr}"] + prev, writes=[f"tmp{2 + pr}"])
                zo = self.os_(j, t)
                S.op("dve", lambda e, zo=zo, pB=pB, ty=ty: e.tensor_tensor(out=zo, in0=pB[:], in1=ty[:], op=ALU.mult),
                     reads=[f"pb{bB}", f"tmp{2 + pr}"], writes=[f"o{j}_{t}"])

    def attn_prep(self):
        S = self.S
        R = self.R
        kT = R[:, R_K:R_K + SEQ]
        kTB = R[:, R_KB:R_KB + SEQ]
        Vt = R[:, R_VT:R_VT + 48 * 192].rearrange("p (b c) -> p b c", c=192)
        S.op("pool", lambda e: e.memset(Vt[:, :, 64:128], 1.0), writes=["Vones"])
        S.op("pool", lambda e: e.memset(kT[64:128, :], 0.0), writes=["kzA"] + [f"o9_{t}" for t in range(NT)])
        S.op("pool", lambda e: e.memset(kTB[0:64, :], 0.0), writes=["kzB"] + [f"o10_{t}" for t in range(NT)])

    def attn_mixer(self):
        S = self.S
        R = self.R
        qT = R[:, R_Q:R_Q + SEQ]
        kT = R[:, R_K:R_K + SEQ]
        kTB = R[:, R_KB:R_KB + SEQ]
        kTs = (kT, kTB)
        vT = R[:, R_V:R_V + SEQ]
        Vt = R[:, R_VT:R_VT + 48 * 192].rearrange("p (b c) -> p b c", c=192)
        Et = R[:, R_E:R_E + 1280].rearrange("p (h c) -> p h c", h=2)
        P1 = [R[:, R_P1 + i * TW: R_P1 + (i + 1) * TW] for i in range(NP1)]
        PP = [R[:, R_P + i * TW: R_P + (i + 1) * TW] for i in range(NPP)]
        q4 = qT.rearrange("p (m r) -> p r m", r=4)
        k4s = [x.rearrange("p (m r) -> p r m", r=4) for x in kTs]
        k16s = [x.rearrange("p (m r) -> p r m", r=16) for x in kTs]
        q16 = qT.rearrange("p (m r) -> p r m", r=16)
        v4 = vT.rearrange("p (m r) -> p r m", r=4)
        v16 = vT.rearrange("p (m r) -> p r m", r=16)
        SK = 48 * GL - 1
        gten = self.gscr.tensor


        for hp in range(8):
            w, wr = self.acquire()
            wv = w[:, 0:3072].rearrange("p (k r n) -> p k r n", k=KC, r=3)

            def eload(e, hp=hp):
                ins = []
                for hh in range(2):
                    h = 2 * hp + hh
                    src = bass.AP(gten, 127 + (h * 3) * GL, [[SK, 128], [GL, 2], [1, 256]])
                    dst = Et[:, hh, 0:512].rearrange("p (b c) -> p b c", b=2)
                    ins.append(e.dma_start(out=dst, in_=src))
                    src = bass.AP(gten, 127 + (h * 3 + 2) * GL, [[SK, 128], [1, 128]])
                    ins.append(e.dma_start(out=Et[:, hh, 512:640], in_=src))
                return ins
            S.dma("sp", eload, reads=["gscr"], writes=["E", "cuh"] + [f"cu{t}" for t in range(NT)],
                  sem_key="E", nstarts=4)

            for hh in range(2):
                e1p_o = self.tab[:, hh * 256:(hh + 1) * 256].rearrange("p (h r j) -> p h r j", h=2, r=4)
                e1p_i = Et[:, hh, 0:256].rearrange("p (h j r) -> p h r j", h=2, r=4)
                S.op("pool", lambda e, e1p_o=e1p_o, e1p_i=e1p_i: e.tensor_copy(out=e1p_o, in_=e1p_i),
                     reads=["E"], writes=[f"E1p{hh}", "OH"] if hh == 0 else [f"E1p{hh}"])

            for t in range(NT):
                for r in range(3):
                    for k in range(KC):
                        self.mm(self.pb[r][:], wv[:, k, r, :], self.hs(k, t), k == 0, k == KC - 1,
                                reads=[wr, f"h{k}_{t}"], writes=[f"pb{r}"])
                sl = slice(t * TW, (t + 1) * TW)
                p0, p1, p2 = self.pb[0], self.pb[1], self.pb[2]
                S.op("act", lambda e, sl=sl, p0=p0: e.activation(out=qT[:, sl], in_=p0[:], func=AF.Copy, scale=0.125),
                     reads=["pb0"], writes=[f"q{t}"])
                S.op("dve", lambda e, sl=sl, p1=p1: e.tensor_copy(out=kT[0:64, sl], in_=p1[0:64, :]),
                     reads=["pb1"], writes=[f"k{t}"])
                S.op("dve", lambda e, sl=sl, p1=p1: e.tensor_copy(out=kTB[64:128, sl], in_=p1[64:128, :]),
                     reads=["pb1"], writes=[f"kb{t}"])
                S.op("act", lambda e, sl=sl, p2=p2: e.activation(out=vT[:, sl], in_=p2[:], func=AF.Copy),
                     reads=["pb2"], writes=[f"v{t}"])

            allv = [f"v{t}" for t in range(NT)]
            ptbufs = [(self.pt[:], "pt"), (self.pb[4][:].bitcast(BF16), "pb4")]

            def emit_tr(lay, half, bufi):
                ptb, ptn = ptbufs[bufi]
                for bi in range(8):
                    blk = half * 8 + bi
                    if lay == 0:
                        src = vT[:, blk * 128:(blk + 1) * 128]
                    elif lay == 1:
                        src = v4[:, blk // 4, (blk % 4) * 128:(blk % 4 + 1) * 128]
                    else:
                        src = v16[:, blk, :]
                    dst = ptb[:, bi * 128:(bi + 1) * 128]
                    S.op("pe", lambda e, dst=dst, src=src: e.transpose(out=dst, in_=src, identity=self.ident[:]),
                         reads=allv + ["ident"], writes=[ptn])
                b0 = lay * 16 + half * 8
                dst = bass.AP(R, R_VT + b0 * 192, [[R_N, 128], [192, 8], [128, 2], [1, 64]])
                src = ptb.rearrange("p (b h c) -> p b h c", b=8, h=2)
                S.op("dve", lambda e, dst=dst, src=src: e.tensor_copy(out=dst, in_=src),
                     reads=[ptn], writes=[f"V{lay}_{half}"])

            tr_groups = [(0, 0), (1, 0), (1, 1), (2, 0), (2, 1), (0, 1)]
            emit_tr(0, 0, 0)

            units = []
            for T in range(NT):
                for hd in range(2):
                    rows = slice(0, 128)
                    kT_ = kTs[hd]
                    k4 = k4s[hd]
                    k16 = k16s[hd]
                    kn = "k" if hd == 0 else "kb"
                    kz = "kzA" if hd == 0 else "kzB"
                    vc = slice(0, 128) if hd == 0 else slice(64, 192)
                    obank = 5 + hd
                    O = self.pb[obank]
                    Oa = O[:].rearrange("p (r a j) -> p r a j", r=4, a=4)
                    O3 = O[:].rearrange("p (r i q) -> p r q i", r=4, q=4)
                    ulist = []
                    for uu in range(2):
                        qk = []
                        pv = []
                        for qq in range(2):
                            m = 4 * T + 2 * uu + qq
                            off = qq * 256
                            qs = qT[rows, m * 128:(m + 1) * 128].rearrange("p (j r) -> p r j", r=4)
                            qk.append((off, 128, kT_[rows, m * 128:(m + 1) * 128], qs, [f"q{m // 4}", f"{kn}{m // 4}", kz]))
                            oc = Oa[:, :, m - 4 * T, :]
                            pv.append((oc, Vt[:, m, vc], off, 128, [f"V0_{m // 8}"]))
                            if m > 0:
                                qk.append((off + 128, 128, kT_[rows, (m - 1) * 128:m * 128], qs,
                                           [f"q{m // 4}", f"{kn}{(m - 1) // 4}"]))
                                pv.append((oc, Vt[:, m - 1, vc], off + 128, 128, [f"V0_{(m - 1) // 8}"]))
                        Eap = self.tab[:, hd * 256:(hd + 1) * 256].unsqueeze(1).to_broadcast([128, 2, 256])
                        ulist.append(dict(qk=qk, pv=pv, E=Eap, kk=128, eshape=(2, 256), ename=f"E1p{hd}", br1=True))
                    for uu in range(2):
                        qk = []
                        pv = []
                        for qq in range(2):
                            r = 2 * uu + qq
                            off = qq * 256
                            qs = q4[rows, r, T * 128:(T + 1) * 128]
                            qk.append((off, 128, k4[rows, r, T * 128:(T + 1) * 128], qs, [f"q{T}", f"{kn}{T}"]))
                            oc = O[:, r * 128:(r + 1) * 128]
                            blk = 16 + r * 4 + T
                            pv.append((oc, Vt[:, blk, vc], off, 128, [f"V1_{(blk - 16) // 8}"]))
                            if T > 0:
                                qk.append((off + 128, 128, k4[rows, r, (T - 1) * 128:T * 128], qs, [f"q{T}", f"{kn}{T - 1}"]))
                                pv.append((oc, Vt[:, blk - 1, vc], off + 128, 128, [f"V1_{(blk - 17) // 8}"]))
                        Eap = Et[:, hd, 256:512].unsqueeze(1).to_broadcast([128, 2, 256])
                        ulist.append(dict(qk=qk, pv=pv, E=Eap, kk=128, eshape=(2, 256)))
                    kk = 128
                    krows = slice(0, 128)
                    qk = []
                    pv = []
                    for r in range(16):
                        qk.append((r * 32, 32, k16[krows, r, 0:kk], q16[rows, r, 32 * T:32 * T + 32],
                                   [f"q{T}"] + [f"{kn}{tt}" for tt in range(NT)]))
                        pv.append((O3[:, r % 4, r // 4, :], Vt[0:kk, 32 + r, vc], r * 32, 32, [f"V2_{r // 8}"]))
                    Eap = Et[0:kk, hd, 512 + 32 * T:512 + 32 * T + 32].unsqueeze(1).to_broadcast([kk, 16, 32])
                    ulist.append(dict(qk=qk, pv=pv, E=Eap, kk=kk, eshape=(16, 32)))
                    for ui, u in enumerate(ulist):
                        u["T"] = T
                        u["hd"] = hd
                        u["obank"] = obank
                        u["first"] = (ui == 0)
                        u["last"] = (ui == len(ulist) - 1)
                        units.append(u)

            def emit_qk(u, ui):
                sb = ui % 4
                u["sb"] = sb
                Sps = self.pb[sb]
                kk = u["kk"]
                for (off, n, lh, rh, rd) in u["qk"]:
                    so = Sps[0:kk, off:off + n]
                    if u.get("br1"):
                        so = so.rearrange("p (r j) -> p r j", r=4)
                    self.mm(so, lh, rh, True, True, reads=rd, writes=[f"pb{sb}"])
                pi = ui % NPP
                u["pi"] = pi
                p1i = ui % NP1
                p1 = P1[p1i]
                S.op("act", lambda e, p1=p1, Sps=Sps, kk=kk: e.activation(out=p1[0:kk, :], in_=Sps[0:kk, :], func=AF.Exp),
                     reads=[f"pb{sb}"], writes=[f"P1_{p1i}"])
                a, b = u["eshape"]
                pp = PP[pi]
                ppv = pp[0:kk, :].rearrange("p (a b) -> p a b", a=a)
                p1v = p1[0:kk, :].rearrange("p (a b) -> p a b", a=a)
                Eap = u["E"]
                eng = "pool" if ui % 2 == 0 else "dve"
                S.op(eng, lambda e, ppv=ppv, p1v=p1v, Eap=Eap: e.tensor_tensor(out=ppv, in0=p1v, in1=Eap, op=ALU.mult),
                     reads=[f"P1_{p1i}", u.get("ename", "E")], writes=[f"P_{pi}"])

            def emit_pv(u):
                ob = u["obank"]
                pp = PP[u["pi"]]
                kk = u["kk"]
                npv = len(u["pv"])
                for i, (oc, lh, off, n, rd) in enumerate(u["pv"]):
                    st = u["first"] and i == 0
                    sp_ = u["last"] and i == npv - 1
                    mv = pp[0:kk, off:off + n]
                    if u.get("br1"):
                        mv = mv.rearrange("p (r j) -> p r j", r=4)
                    self.mm(oc, lh, mv, st, sp_, reads=rd + [f"P_{u['pi']}", "Vones"],
                            writes=[f"pb{ob}"], skip=True)
                if u["last"]:
                    T, hd = u["T"], u["hd"]
                    O = self.pb[ob]
                    rc = self.tmp[(T % 2) * 2 + hd]
                    rn = f"tmp{(T % 2) * 2 + hd}"
                    if hd == 0:
                        den, num, orow = O[64:128, :], O[0:64, :], slice(0, 64)
                    else:
                        den, num, orow = O[0:64, :], O[64:128, :], slice(64, 128)
                    rcv = rc[orow, :]
                    S.op("act", lambda e, rcv=rcv, den=den: e.activation(out=rcv, in_=den, func=AF.Ln),
                         reads=[f"pb{ob}"], writes=[rn])
                    S.op("act", lambda e, rcv=rcv: e.activation(out=rcv, in_=rcv, func=AF.Exp, scale=-1.0),
                         reads=[rn], writes=[rn])
                    oo = self.os_(hp, T)[orow, :].rearrange("p (i r) -> p r i", r=4)
                    numv = num.rearrange("p (r i) -> p r i", r=4)
                    rcv3 = rcv.rearrange("p (r i) -> p r i", r=4)
                    S.op("dve", lambda e, oo=oo, numv=numv, rcv3=rcv3: e.tensor_tensor(out=oo, in0=numv, in1=rcv3, op=ALU.mult),
                         reads=[f"pb{ob}", rn], writes=[f"o{hp}_{T}"])

            SKEW = 4
            nu = len(units)
            for i in range(nu + SKEW):
                if i < nu:
                    emit_qk(units[i], i)
                if i < 5:
                    emit_tr(tr_groups[i + 1][0], tr_groups[i + 1][1], (i + 1) % 2)
                if i >= SKEW:
                    emit_pv(units[i - SKEW])

    def build(self):
        self.plan_weights()
        self.prologue()
        self._issue_w(0)
        self.wissued = 1
        first = True
        for l in self.layers:
            if l % 2 == 1:
                self.run_side(1000)
                self.attn_prep()
            self.norm(2 * l)
            if first and any(ll % 2 == 1 for ll in self.layers):
                self.build_tables()
                if l % 2 == 1:
                    self.run_side(1000)
            first = False
            if l % 2 == 0:
                self.conv_mixer(l // 2)
            else:
                self.attn_mixer()
            self.out_proj()
            self.norm(2 * l + 1)
            self.ffn()
        if self.final:
            self.norm(8, final=True)
        S = self.S
        keys = self.out_keys
        if not self.final:
            for c in range(KC):
                src = self.xT[:, c * SEQ:(c + 1) * SEQ]
                dst = self.yd[:, c * SEQ:(c + 1) * SEQ]
                S.dma("sp", lambda e, dst=dst, src=src: e.dma_start(out=dst, in_=src),
                      reads=[f"x{c}_{t}" for t in range(NT)], sem_key=f"yo{c}")
                keys.append(f"yo{c}")
        S.emit(self.nc, final_wait_keys=keys)
        return self.nc


def _t5_bucket(dist):
    exact = 16
    df = np.maximum(dist, 1).astype(np.float32)
    large = exact + (np.log(df / np.float32(exact)) / np.float32(math.log(2048 / exact))
                     * np.float32(32 - exact)).astype(np.int32)
    large = np.minimum(large, 31)
    return np.where(dist < exact, dist, large)


def _const_tables():
    oh = np.zeros((33, 3, GL), np.float32)
    for br, dil in enumerate((1, 4, 16)):
        for u in range(GL):
            rel = u - 127
            if 0 <= rel <= 128:
                b = int(_t5_bucket(np.array([rel * dil], np.int32))[0])
                oh[b, br, u] = 1.0
            else:
                oh[32, br, u] = -30000.0
    return oh.reshape(33, 3 * GL)


def _kmaj(w):
    K = w.shape[0] // 128
    return w.reshape(K, 128, w.shape[1]).transpose(1, 0, 2)


def prep_shared(inp, layers=(0, 1, 2, 3)):
    m = {}
    g = np.concatenate([np.stack([inp["mix_norm"][l], inp["ffn_norm"][l]]) for l in range(NL)]
                       + [inp["final_norm"][None]], axis=0)
    m["gains"] = np.ascontiguousarray(g.reshape(9, 8, 128).transpose(2, 0, 1).reshape(128, 72))
    ck = inp["conv_kernel"]
    m["ck"] = np.ascontiguousarray(ck.reshape(2, 3, 8, 128).transpose(3, 0, 2, 1).reshape(128, 48))
    m["ident"] = np.eye(128, dtype=np.float32)
    m["oh"] = _const_tables()
    m["rbx"] = np.concatenate([inp["rel_bias"], np.ones((1, 16), np.float32)], axis=0)
    for l in layers:
        j = l // 2
        wi = inp["conv_w_in"][j] if l % 2 == 0 else inp["attn_w_qkv"][j]
        wo = inp["conv_w_out"][j] if l % 2 == 0 else inp["attn_w_out"][j]
        a = _kmaj(wi).reshape(128, 8, 3, 8, 128)
        m[f"wmix{l}"] = np.ascontiguousarray(a.transpose(3, 0, 1, 2, 4).reshape(8, 128, 3072))
        a = _kmaj(wo).reshape(128, 8, 4, 2, 128)
        m[f"wout{l}"] = np.ascontiguousarray(a.transpose(2, 0, 3, 1, 4).reshape(4, 128, 2048))
        ag = _kmaj(inp["ffn_w_gate"][l]).reshape(128, 8, 22, 128)
        au = _kmaj(inp["ffn_w_up"][l]).reshape(128, 8, 22, 128)
        a = np.stack([ag, au], axis=2)
        m[f"wgu{l}"] = np.ascontiguousarray(a.transpose(3, 0, 1, 2, 4).reshape(22, 128, 2048))
        a = _kmaj(inp["ffn_w_down"][l]).reshape(128, 2, FG, 4, 2, 128)
        m[f"wdn{l}"] = np.ascontiguousarray(a.transpose(1, 3, 0, 4, 2, 5).reshape(8, 128, 2816))
    return m


def x_to_dev(xb):
    return np.ascontiguousarray(xb.T.reshape(8, 128, SEQ).transpose(1, 0, 2).reshape(128, 8 * SEQ))


def y_from_dev(y):
    return np.ascontiguousarray(y.reshape(128, 8, SEQ).transpose(1, 0, 2).reshape(D, SEQ).T)


def kernel(**inputs):
    inp = {k: np.asarray(v, dtype=np.float32) for k, v in inputs.items()}
    shared = prep_shared(inp)
    nc = Builder().build()
    nb = inp["x"].shape[0]
    in_maps = []
    for b in range(nb):
        m = dict(shared)
        m["x"] = x_to_dev(inp["x"][b])
        in_maps.append(m)
    res = run_bass_kernel_spmd(nc, in_maps, core_ids=list(range(nb)))
    out = np.stack([y_from_dev(np.asarray(res.results[b]["y"], dtype=np.float32)) for b in range(nb)], axis=0)
    return out.astype(np.float32)
```

```python
import math
import numpy as np
import concourse.bass as bass
import concourse.mybir as mybir
from concourse.bass_utils import run_bass_kernel_spmd

F32 = mybir.dt.float32
BF16 = mybir.dt.bfloat16
AF = mybir.ActivationFunctionType
ALU = mybir.AluOpType

D = 1024
SEQ = 2048
DFF = 2816
NL = 4
NT = 4
TW = 512
KC = 8
FG = 11
GL = 384
WSLOT = 3072
NSLOT = 3

ENGS = ("pe", "act", "dve", "pool", "sp")
SEM_K = 6
SEM_CH = 512


class Op:
    __slots__ = ("eng", "fn", "idx", "rdeps", "wdeps", "is_dma", "sem_key", "val",
                 "needs_signal", "sig", "waits", "nstarts")


class Sched:
    def __init__(self):
        self.ops = {e: [] for e in ENGS}
        self.lastw = {}
        self.readers = {}
        self.dma_count = {}

    def _add(self, eng, fn, reads, writes, is_dma=False, sem_key=None, nstarts=1):
        o = Op()
        o.eng = eng
        o.fn = fn
        o.idx = len(self.ops[eng])
        o.is_dma = is_dma
        o.sem_key = sem_key
        o.needs_signal = False
        o.sig = -1
        o.nstarts = nstarts
        o.val = 0
        if is_dma:
            c = self.dma_count.get(sem_key, 0) + nstarts
            self.dma_count[sem_key] = c
            o.val = 16 * c
        rdeps = []
        wdeps = []
        for r in reads:
            w = self.lastw.get(r)
            if w is not None:
                rdeps.append(w)
        for r in writes:
            w = self.lastw.get(r)
            if w is not None:
                wdeps.append(w)
            rd = self.readers.get(r)
            if rd:
                for x in rd.values():
                    if isinstance(x, list):
                        wdeps.extend(x)
                    else:
                        wdeps.append(x)
        for r in reads:
            d = self.readers.setdefault(r, {})
            if is_dma:
                d.setdefault("dma", []).append(o)
            else:
                d[eng] = o
        for r in writes:
            self.lastw[r] = o
            self.readers[r] = {}
        o.rdeps = rdeps
        o.wdeps = wdeps
        self.ops[eng].append(o)
        return o

    def op(self, eng, fn, reads=(), writes=()):
        return self._add(eng, fn, reads, writes)

    def dma(self, queue, fn, reads=(), writes=(), sem_key=None, nstarts=1):
        return self._add(queue, fn, reads, writes, True, sem_key, nstarts)

    def resolve(self):
        for e in ENGS:
            seen_idx = {x: -1 for x in ENGS}
            seen_dma = {}
            for o in self.ops[e]:
                best = {}
                final = []
                deps = [(d, True) for d in o.rdeps] + [(d, False) for d in o.wdeps]
                for d, is_raw in deps:
                    if d is o:
                        continue
                    if d.is_dma:
                        if seen_dma.get(d.sem_key, 0) >= d.val:
                            continue
                        seen_dma[d.sem_key] = d.val
                        final.append(d)
                        continue
                    if d.eng == e:
                        if e == "pe":
                            continue
                        if not (is_raw or o.is_dma):
                            continue
                    if seen_idx[d.eng] >= d.idx:
                        continue
                    seen_idx[d.eng] = d.idx
                    b = best.get(d.eng)
                    if b is None or d.idx > b.idx:
                        best[d.eng] = d
                for d in best.values():
                    d.needs_signal = True
                    final.append(d)
                o.waits = final
        for e in ENGS:
            n = 0
            for o in self.ops[e]:
                if o.needs_signal and not o.is_dma:
                    o.sig = n
                    n += 1

    @staticmethod
    def sem_slot_val(sig):
        c = sig // SEM_CH
        slot = c % SEM_K
        val = (c // SEM_K) * SEM_CH + (sig % SEM_CH) + 1
        return slot, val

    def emit(self, nc, final_wait_keys=()):
        self.resolve()
        import contextlib
        with contextlib.ExitStack() as st:
            esems = {}
            for e in ENGS:
                esems[e] = [st.enter_context(nc.semaphore(f"s_{e}_{i}")) for i in range(SEM_K)]
            dsems = {}
            for k in self.dma_count:
                dsems[k] = st.enter_context(nc.semaphore(f"d_{k}"))
            block = st.enter_context(nc.Block())

            def run(e, eng):
                for o in self.ops[e]:
                    for d in o.waits:
                        if d.is_dma:
                            eng.wait_ge(dsems[d.sem_key], d.val)
                        else:
                            slot, val = self.sem_slot_val(d.sig)
                            eng.wait_ge(esems[d.eng][slot], val)
                    r = o.fn(eng)
                    if o.is_dma:
                        if not isinstance(r, (list, tuple)):
                            r = [r]
                        assert len(r) == o.nstarts, (len(r), o.nstarts)
                        for ins in r:
                            ins.then_inc(dsems[o.sem_key], 16)
                    elif o.needs_signal:
                        slot, _ = self.sem_slot_val(o.sig)
                        r.then_inc(esems[e][slot], 1)
                if e == "sp":
                    for k in final_wait_keys:
                        eng.wait_ge(dsems[k], 16 * self.dma_count[k])

            @block.tensor
            def _(eng):
                run("pe", eng)

            @block.scalar
            def _(eng):
                run("act", eng)

            @block.vector
            def _(eng):
                run("dve", eng)

            @block.gpsimd
            def _(eng):
                run("pool", eng)

            @block.sync
            def _(eng):
                run("sp", eng)


R_O = 0
R_Q = 16384
R_K = 18432
R_KB = 20480
R_V = 22528
R_VT = 24576
R_E = 33792
NP1 = 3
NPP = 5
R_P1 = 35072
R_P = R_P1 + NP1 * 512
R_N = R_P + NPP * 512


class Builder:
    def __init__(self, layers=(0, 1, 2, 3), final=True):
        self.layers = tuple(layers)
        self.final = final
        nc = self.nc = bass.Bass("TRN2", target_bir_lowering=False)
        self.S = Sched()
        dt = nc.dram_tensor
        self.xd = dt("x", [128, KC * SEQ], F32, kind="ExternalInput").ap()
        self.gd = dt("gains", [128, 72], F32, kind="ExternalInput").ap()
        self.ckd = dt("ck", [128, 48], F32, kind="ExternalInput").ap()
        self.idd = dt("ident", [128, 128], F32, kind="ExternalInput").ap()
        self.ohd = dt("oh", [33, 3 * GL], F32, kind="ExternalInput").ap()
        self.rbd = dt("rbx", [33, 16], F32, kind="ExternalInput").ap()
        self.wmix = {}
        self.wout = {}
        self.wgu = {}
        self.wdn = {}
        for l in self.layers:
            self.wmix[l] = dt(f"wmix{l}", [8, 128, 3072], F32, kind="ExternalInput").ap()
            self.wout[l] = dt(f"wout{l}", [4, 128, 2048], F32, kind="ExternalInput").ap()
            self.wgu[l] = dt(f"wgu{l}", [22, 128, 2048], F32, kind="ExternalInput").ap()
            self.wdn[l] = dt(f"wdn{l}", [8, 128, 2816], F32, kind="ExternalInput").ap()
        self.yd = dt("y", [128, KC * SEQ], F32, kind="ExternalOutput").ap()
        self.gscr = dt("gscr", [128, 48 * GL], BF16).ap()

        a = nc.alloc_sbuf_tensor
        self.xT = a("xT", [128, KC * SEQ], F32)
        self.hT = a("hT", [128, KC * SEQ], BF16)
        self.R = a("R", [128, R_N], BF16)
        self.wsl = [a(f"wsl{i}", [128, WSLOT], BF16) for i in range(NSLOT)]
        self.rs = [a(f"rs{i}", [128, TW], F32) for i in range(2)]
        self.tmp = [a(f"tmp{i}", [128, TW], F32) for i in range(4)]
        self.sq = [a(f"sq{i}", [128, TW], BF16) for i in range(2)]
        self.gains = a("gains_sb", [128, 72], F32)
        self.bar = a("bar_sb", [128, 20], F32)
        self.tab = a("tab_sb", [128, 1152], BF16)
        self.ck = a("ck_sb", [128, 48], F32)
        self.ident = a("ident_sb", [128, 128], BF16)
        self.ones = a("ones_sb", [128, 128], BF16)
        self.pb = [nc.alloc_psum_tensor(f"pb{i}", [128, TW], F32) for i in range(7)]
        self.pt = nc.alloc_psum_tensor("pt", [128, 1024], BF16)

        print("sbuf remaining", nc.sbuf_bytes_remaining)
        self.out_keys = []
        self.side_jobs = []
        self.wlist = []
        self.wissued = 0
        self.wptr = 0
        self.cnt = 0

    def xs(self, c, t):
        return self.xT[:, c * SEQ + t * TW: c * SEQ + (t + 1) * TW]

    def hs(self, c, t):
        return self.hT[:, c * SEQ + t * TW: c * SEQ + (t + 1) * TW]

    def os_(self, c, t):
        return self.R[:, R_O + c * SEQ + t * TW: R_O + c * SEQ + (t + 1) * TW]

    def mm(self, out, lhsT, rhs, start, stop, reads, writes, skip=False):
        if skip:
            fn = lambda e: e.matmul(out, lhsT=lhsT, rhs=rhs, start=start, stop=stop, skip_group_check=True)
        else:
            fn = lambda e: e.matmul(out, lhsT=lhsT, rhs=rhs, start=start, stop=stop)
        return self.S.op("pe", fn, reads=reads, writes=writes)

    def plan_weights(self):
        for l in self.layers:
            for j in range(8):
                self.wlist.append((self.wmix[l][j], 3072))
            for half in range(2):
                for i in range(4):
                    self.wlist.append((self.wout[l][i], 2048))
            for g in range(2):
                for fl in range(FG):
                    self.wlist.append((self.wgu[l][g * FG + fl], 2048))
                for half in range(2 if g == 1 else 1):
                    for i in range(4):
                        self.wlist.append((self.wdn[l][g * 4 + i], 2816))

    def _issue_w(self, i):
        src, L = self.wlist[i]
        slot = i % NSLOT
        dst = self.wsl[slot][:, 0:L]
        self.S.dma("pool", lambda e: e.dma_start(out=dst, in_=src), writes=[f"w{slot}"], sem_key=f"w{slot}")

    def acquire(self):
        i = self.wptr
        while self.wissued < min(len(self.wlist), i + NSLOT):
            self._issue_w(self.wissued)
            self.wissued += 1
        self.wptr += 1
        slot = i % NSLOT
        return self.wsl[slot], f"w{slot}"

    def prologue(self):
        S = self.S
        S.dma("sp", lambda e: e.dma_start(out=self.gains[:], in_=self.gd), writes=["gains"], sem_key="misc")
        S.dma("sp", lambda e: e.dma_start(out=self.ck[:], in_=self.ckd), writes=["ck"], sem_key="misc2")
        S.dma("pool", lambda e: e.dma_start(out=self.ident[:], in_=self.idd), writes=["ident"], sem_key="ident")
        S.op("dve", lambda e: e.memset(self.ones[:], 1.0), writes=["ones"])
        xv = self.xT[:].rearrange("p (c t w) -> p c t w", c=KC, t=NT)
        xdv = self.xd.rearrange("p (c t w) -> p c t w", c=KC, t=NT)
        for t in range(NT):
            dst = xv[:, :, t, :]
            src = xdv[:, :, t, :]
            S.dma("sp", lambda e, dst=dst, src=src: e.dma_start(out=dst, in_=src),
                  writes=[f"x{c}_{t}" for c in range(KC)], sem_key=f"xl{t}")

    def build_tables(self):
        S = self.S
        G = self.R[:, R_Q:R_Q + 48 * GL]
        RBs = [self.rs[0][:].bitcast(BF16), self.rs[1][:].bitcast(BF16)]
        RBX = self.bar[:, 4:20]
        OH = self.tab[:, 0:1152]
        S.dma("pool", lambda e: e.dma_start(out=OH[0:33, :], in_=self.ohd), writes=["OH"], sem_key="oh")
        S.dma("sp", lambda e: e.dma_start(out=RBX[0:33, :], in_=self.rbd), writes=["RBX"], sem_key="rbx")
        for h in range(16):
            dst = RBs[h // 8][0:33, (h % 8) * 128:(h % 8 + 1) * 128]
            src = RBX[0:33, h:h + 1].to_broadcast([33, 128])
            S.op("pool", lambda e, dst=dst, src=src: e.tensor_copy(out=dst, in_=src),
                 reads=["RBX"], writes=[f"rs{h // 8}"])
        def job(h, br, n):
            def run():
                ps = self.pt[:].bitcast(F32)[:, 0:GL]
                self.mm(ps, RBs[h // 8][0:33, (h % 8) * 128:(h % 8 + 1) * 128], OH[0:33, br * GL:(br + 1) * GL],
                        True, True, reads=[f"rs{h // 8}", "OH"], writes=["pt"])
                dst = G[:, n * GL:(n + 1) * GL]
                S.op("act", lambda e, dst=dst, ps=ps: e.activation(out=dst, in_=ps, func=AF.Exp),
                     reads=["pt"], writes=["G"])
            return run

        def finish():
            self._tables_finish(G)

        jobs = []
        n = 0
        for h in range(16):
            for br in range(3):
                jobs.append(job(h, br, n))
                n += 1
        jobs.append(finish)
        self.side_jobs = jobs

    def run_side(self, k):
        for _ in range(k):
            if self.side_jobs:
                self.side_jobs.pop(0)()

    def _tables_finish(self, G):
        S = self.S
        S.dma("sp", lambda e: e.dma_start(out=self.gscr, in_=G), reads=["G"], writes=["gscr"], sem_key="gscr")
        names = ["G", "Vones", "E", "kzA", "kzB"]
        names += [f"o{c}_{t}" for c in range(8, FG) for t in range(NT)]
        for t in range(NT):
            names += [f"q{t}", f"k{t}", f"kb{t}", f"v{t}"]
        names += [f"V{lay}_{half}" for lay in range(3) for half in range(2)]
        names += [f"P1_{i}" for i in range(NP1)] + [f"P_{i}" for i in range(NPP)]
        S.dma("sp", lambda e: e.dma_start(out=self.bar[0:1, 0:4], in_=self.gd[0:1, 0:4]), writes=names, sem_key="bar")

    def norm(self, gi, final=False):
        S = self.S
        for t in range(NT):
            ss = self.pb[6]
            for c in range(KC):
                sq = self.sq[c % 2]
                xin = self.xs(c, t)
                S.op("act", lambda e, sq=sq, xin=xin: e.activation(out=sq[:], in_=xin, func=AF.Square),
                     reads=[f"x{c}_{t}"], writes=[f"sq{c % 2}"])
                self.mm(ss[:], self.ones[:], sq[:], c == 0, c == KC - 1, reads=[f"sq{c % 2}", "ones"], writes=["pb6"])
            rs = self.rs[t % 2]
            S.op("dve", lambda e, rs=rs, ss=ss: e.tensor_scalar(out=rs[:], in0=ss[:], scalar1=1.0 / D, scalar2=1e-6,
                                                              op0=ALU.mult, op1=ALU.add),
                 reads=["pb6"], writes=[f"rs{t % 2}"])
            S.op("act", lambda e, rs=rs: e.activation(out=rs[:], in_=rs[:], func=AF.Ln),
                 reads=[f"rs{t % 2}"], writes=[f"rs{t % 2}"])
            S.op("act", lambda e, rs=rs: e.activation(out=rs[:], in_=rs[:], func=AF.Exp, scale=-0.5),
                 reads=[f"rs{t % 2}"], writes=[f"rs{t % 2}"])
            for c in range(KC):
                xin = self.xs(c, t)
                g = self.gains[:, gi * 8 + c: gi * 8 + c + 1]
                if final:
                    S.op("dve", lambda e, xin=xin, g=g, rs=rs: e.scalar_tensor_tensor(
                        out=xin, in0=xin, scalar=g, in1=rs[:], op0=ALU.mult, op1=ALU.mult),
                        reads=[f"x{c}_{t}", f"rs{t % 2}", "gains"], writes=[f"x{c}_{t}"])
                    if c == KC - 1:
                        xv = self.xT[:].rearrange("p (c t w) -> p c t w", c=KC, t=NT)[:, :, t, :]
                        yv = self.yd.rearrange("p (c t w) -> p c t w", c=KC, t=NT)[:, :, t, :]
                        S.dma("sp", lambda e, xv=xv, yv=yv: e.dma_start(out=yv, in_=xv),
                              reads=[f"x{cc}_{t}" for cc in range(KC)], sem_key=f"yo{t}")
                        self.out_keys.append(f"yo{t}")
                else:
                    ho = self.hs(c, t)
                    if True:
                        S.op("dve", lambda e, xin=xin, g=g, rs=rs, ho=ho: e.scalar_tensor_tensor(
                            out=ho, in0=xin, scalar=g, in1=rs[:], op0=ALU.mult, op1=ALU.mult),
                            reads=[f"x{c}_{t}", f"rs{t % 2}", "gains"], writes=[f"h{c}_{t}"])
                    else:
                        pt_ = self.ptmp
                        S.op("pool", lambda e, xin=xin, g=g, pt_=pt_: e.tensor_scalar(
                            out=pt_[:], in0=xin, scalar1=g, scalar2=None, op0=ALU.mult),
                            reads=[f"x{c}_{t}", "gains"], writes=["ptmp"])
                        S.op("pool", lambda e, rs=rs, ho=ho, pt_=pt_: e.tensor_tensor(
                            out=ho, in0=pt_[:], in1=rs[:], op=ALU.mult),
                            reads=["ptmp", f"rs{t % 2}"], writes=[f"h{c}_{t}"])

    def out_proj(self):
        S = self.S
        for half, i in [(h_, i_) for h_ in range(2) for i_ in range(4)]:
            w, wr = self.acquire()
            wv = w[:, 0:2048].rearrange("p (nn k n) -> p nn k n", nn=2, k=KC)
            for nn in range(2):
                n = 2 * i + nn
                for t in range(2 * half, 2 * half + 2):
                    b = self.cnt % 5
                    self.cnt += 1
                    ps = self.pb[b]
                    for k in range(KC):
                        self.mm(ps[:], wv[:, nn, k, :], self.os_(k, t), k == 0, k == KC - 1,
                                reads=[wr, f"o{k}_{t}"], writes=[f"pb{b}"])
                    xo = self.xs(n, t)
                    S.op("dve", lambda e, xo=xo, ps=ps: e.tensor_tensor(out=xo, in0=ps[:], in1=xo, op=ALU.add),
                         reads=[f"pb{b}", f"x{n}_{t}"], writes=[f"x{n}_{t}"])

    def ffn(self):
        S = self.S
        for g in range(2):
            for fl in range(FG):
                w, wr = self.acquire()
                wv = w[:, 0:2048].rearrange("p (k r n) -> p k r n", k=KC, r=2)
                for t in range(NT):
                    pr = self.cnt % 2
                    self.cnt += 1
                    bG, bU = 2 * pr, 2 * pr + 1
                    for k in range(KC):
                        self.mm(self.pb[bG][:], wv[:, k, 0, :], self.hs(k, t), k == 0, k == KC - 1,
                                reads=[wr, f"h{k}_{t}"], writes=[f"pb{bG}"])
                    for k in range(KC):
                        self.mm(self.pb[bU][:], wv[:, k, 1, :], self.hs(k, t), k == 0, k == KC - 1,
                                reads=[wr, f"h{k}_{t}"], writes=[f"pb{bU}"])
                    tm = self.tmp[pr]
                    pG = self.pb[bG]
                    pU = self.pb[bU]
                    S.op("act", lambda e, tm=tm, pG=pG: e.activation(out=tm[:], in_=pG[:], func=AF.Silu),
                         reads=[f"pb{bG}"], writes=[f"tmp{pr}"])
                    ao = self.os_(fl, t) if True else None
                    S.op("dve", lambda e, ao=ao, pU=pU, tm=tm: e.tensor_tensor(out=ao, in0=pU[:], in1=tm[:], op=ALU.mult),
                         reads=[f"pb{bU}", f"tmp{pr}"], writes=[f"o{fl}_{t}"])
            passes = [(None, i_) for i_ in range(4)] if g == 0 else [(h_, i_) for h_ in range(2) for i_ in range(4)]
            for half, i in passes:
                w, wr = self.acquire()
                wv = w[:, 0:2816].rearrange("p (nn f n) -> p nn f n", nn=2, f=FG)
                for nn in range(2):
                    n = 2 * i + nn
                    for t in (range(NT) if half is None else range(2 * half, 2 * half + 2)):
                        b = 4 + self.cnt % 2
                        self.cnt += 1
                        ps = self.pb[b]
                        for fl in range(FG):
                            self.mm(ps[:], wv[:, nn, fl, :], self.os_(fl, t), fl == 0, fl == FG - 1,
                                    reads=[wr, f"o{fl}_{t}"], writes=[f"pb{b}"])
                        xo = self.xs(n, t)
                        S.op("dve", lambda e, xo=xo, ps=ps: e.tensor_tensor(out=xo, in0=ps[:], in1=xo, op=ALU.add),
                             reads=[f"pb{b}", f"x{n}_{t}"], writes=[f"x{n}_{t}"])

    def conv_mixer(self, l2):
        S = self.S
        cu = self.R[:, R_N - 4224:R_N].bitcast(F32)
        S.op("dve", lambda e: e.memset(cu[:, 0:2], 0.0), writes=["cuh"])
        for j in range(8):
            w, wr = self.acquire()
            wv = w[:, 0:3072].rearrange("p (k r n) -> p k r n", k=KC, r=3)
            for t in range(NT):
                pr = self.cnt % 2
                self.cnt += 1
                bC, bU = 2 * pr, 2 * pr + 1
                bB = 4 + self.cnt % 3
                for r, b in ((2, bU), (1, bC), (0, bB)):
                    for k in range(KC):
                        self.mm(self.pb[b][:], wv[:, k, r, :], self.hs(k, t), k == 0, k == KC - 1,
                                reads=[wr, f"h{k}_{t}"], writes=[f"pb{b}"])
                self.run_side(2)
                tu = self.tmp[pr]
                ty = self.tmp[2 + pr]
                pB, pC, pU = self.pb[bB], self.pb[bC], self.pb[bU]
                S.op("act", lambda e, tu=tu, pU=pU: e.activation(out=tu[:], in_=pU[:], func=AF.Copy),
                     reads=[f"pb{bU}"], writes=[f"tmp{pr}"])
                c0 = cu[:, 2 + t * TW: 2 + (t + 1) * TW]
                c1 = cu[:, 1 + t * TW: 1 + (t + 1) * TW]
                c2 = cu[:, t * TW: (t + 1) * TW]
                S.op("dve", lambda e, c0=c0, pC=pC, tu=tu: e.tensor_tensor(out=c0, in0=pC[:], in1=tu[:], op=ALU.mult),
                     reads=[f"pb{bC}", f"tmp{pr}"], writes=[f"cu{t}"])
                kb = (l2 * 8 + j) * 3
                k0 = self.ck[:, kb:kb + 1]
                k1 = self.ck[:, kb + 1:kb + 2]
                k2 = self.ck[:, kb + 2:kb + 3]
                prev = [f"cu{t - 1}"] if t > 0 else ["cuh"]
                S.op("act", lambda e, ty=ty, c0=c0, k2=k2: e.activation(out=ty[:], in_=c0, func=AF.Copy, scale=k2),
                     reads=[f"cu{t}", "ck"], writes=[f"tmp{2 + pr}"])
                S.op("dve", lambda e, ty=ty, c1=c1, k1=k1: e.scalar_tensor_tensor(out=ty[:], in0=c1, scalar=k1, in1=ty[:],
                                                                                op0=ALU.mult, op1=ALU.add),
                     reads=[f"cu{t}", f"tmp{2 + pr}"] + prev, writes=[f"tmp{2 + pr}"])
                S.op("dve", lambda e, ty=ty, c2=c2, k0=k0: e.scalar_tensor_tensor(out=ty[:], in0=c2, scalar=k0, in1=ty[:],
                                                                                op0=ALU.mult, op1=ALU.add),
                     reads=[f"cu{t}", f"tmp{2 + pr}"] + prev, writes=[f"tmp{2 + pr}"])
                zo = self.os_(j, t)
                S.op("dve", lambda e, zo=zo, pB=pB, ty=ty: e.tensor_tensor(out=zo, in0=pB[:], in1=ty[:], op=ALU.mult),
                     reads=[f"pb{bB}", f"tmp{2 + pr}"], writes=[f"o{j}_{t}"])

    def attn_prep(self):
        S = self.S
        R = self.R
        kT = R[:, R_K:R_K + SEQ]
        kTB = R[:, R_KB:R_KB + SEQ]
        Vt = R[:, R_VT:R_VT + 48 * 192].rearrange("p (b c) -> p b c", c=192)
        S.op("pool", lambda e: e.memset(Vt[:, :, 64:128], 1.0), writes=["Vones"])
        S.op("pool", lambda e: e.memset(kT[64:128, :], 0.0), writes=["kzA"] + [f"o9_{t}" for t in range(NT)])
        S.op("pool", lambda e: e.memset(kTB[0:64, :], 0.0), writes=["kzB"] + [f"o10_{t}" for t in range(NT)])

    def attn_mixer(self):
        S = self.S
        R = self.R
        qT = R[:, R_Q:R_Q + SEQ]
        kT = R[:, R_K:R_K + SEQ]
        kTB = R[:, R_KB:R_KB + SEQ]
        kTs = (kT, kTB)
        vT = R[:, R_V:R_V + SEQ]
        Vt = R[:, R_VT:R_VT + 48 * 192].rearrange("p (b c) -> p b c", c=192)
        Et = R[:, R_E:R_E + 1280].rearrange("p (h c) -> p h c", h=2)
        P1 = [R[:, R_P1 + i * TW: R_P1 + (i + 1) * TW] for i in range(NP1)]
        PP = [R[:, R_P + i * TW: R_P + (i + 1) * TW] for i in range(NPP)]
        q4 = qT.rearrange("p (m r) -> p r m", r=4)
        k4s = [x.rearrange("p (m r) -> p r m", r=4) for x in kTs]
        k16s = [x.rearrange("p (m r) -> p r m", r=16) for x in kTs]
        q16 = qT.rearrange("p (m r) -> p r m", r=16)
        v4 = vT.rearrange("p (m r) -> p r m", r=4)
        v16 = vT.rearrange("p (m r) -> p r m", r=16)
        SK = 48 * GL - 1
        gten = self.gscr.tensor


        for hp in range(8):
            w, wr = self.acquire()
            wv = w[:, 0:3072].rearrange("p (k r n) -> p k r n", k=KC, r=3)

            def eload(e, hp=hp):
                ins = []
                for hh in range(2):
                    h = 2 * hp + hh
                    src = bass.AP(gten, 127 + (h * 3) * GL, [[SK, 128], [GL, 2], [1, 256]])
                    dst = Et[:, hh, 0:512].rearrange("p (b c) -> p b c", b=2)
                    ins.append(e.dma_start(out=dst, in_=src))
                    src = bass.AP(gten, 127 + (h * 3 + 2) * GL, [[SK, 128], [1, 128]])
                    ins.append(e.dma_start(out=Et[:, hh, 512:640], in_=src))
                return ins
            S.dma("sp", eload, reads=["gscr"], writes=["E", "cuh"] + [f"cu{t}" for t in range(NT)],
                  sem_key="E", nstarts=4)

            for t in range(NT):
                for r in range(3):
                    for k in range(KC):
                        self.mm(self.pb[r][:], wv[:, k, r, :], self.hs(k, t), k == 0, k == KC - 1,
                                reads=[wr, f"h{k}_{t}"], writes=[f"pb{r}"])
                sl = slice(t * TW, (t + 1) * TW)
                p0, p1, p2 = self.pb[0], self.pb[1], self.pb[2]
                S.op("act", lambda e, sl=sl, p0=p0: e.activation(out=qT[:, sl], in_=p0[:], func=AF.Copy, scale=0.125),
                     reads=["pb0"], writes=[f"q{t}"])
                S.op("dve", lambda e, sl=sl, p1=p1: e.tensor_copy(out=kT[0:64, sl], in_=p1[0:64, :]),
                     reads=["pb1"], writes=[f"k{t}"])
                S.op("dve", lambda e, sl=sl, p1=p1: e.tensor_copy(out=kTB[64:128, sl], in_=p1[64:128, :]),
                     reads=["pb1"], writes=[f"kb{t}"])
                S.op("act", lambda e, sl=sl, p2=p2: e.activation(out=vT[:, sl], in_=p2[:], func=AF.Copy),
                     reads=["pb2"], writes=[f"v{t}"])

            allv = [f"v{t}" for t in range(NT)]
            ptbufs = [(self.pt[:], "pt"), (self.pb[4][:].bitcast(BF16), "pb4")]

            def emit_tr(lay, half, bufi):
                ptb, ptn = ptbufs[bufi]
                for bi in range(8):
                    blk = half * 8 + bi
                    if lay == 0:
                        src = vT[:, blk * 128:(blk + 1) * 128]
                    elif lay == 1:
                        src = v4[:, blk // 4, (blk % 4) * 128:(blk % 4 + 1) * 128]
                    else:
                        src = v16[:, blk, :]
                    dst = ptb[:, bi * 128:(bi + 1) * 128]
                    S.op("pe", lambda e, dst=dst, src=src: e.transpose(out=dst, in_=src, identity=self.ident[:]),
                         reads=allv + ["ident"], writes=[ptn])
                b0 = lay * 16 + half * 8
                dst = bass.AP(R, R_VT + b0 * 192, [[R_N, 128], [192, 8], [128, 2], [1, 64]])
                src = ptb.rearrange("p (b h c) -> p b h c", b=8, h=2)
                S.op("dve", lambda e, dst=dst, src=src: e.tensor_copy(out=dst, in_=src),
                     reads=[ptn], writes=[f"V{lay}_{half}"])

            tr_groups = [(0, 0), (1, 0), (1, 1), (2, 0), (2, 1), (0, 1)]
            emit_tr(0, 0, 0)

            units = []
            for T in range(NT):
                for hd in range(2):
                    rows = slice(0, 128)
                    kT_ = kTs[hd]
                    k4 = k4s[hd]
                    k16 = k16s[hd]
                    kn = "k" if hd == 0 else "kb"
                    kz = "kzA" if hd == 0 else "kzB"
                    vc = slice(0, 128) if hd == 0 else slice(64, 192)
                    obank = 5 + hd
                    O = self.pb[obank]
                    O4 = O[:].rearrange("p (m r) -> p r m", r=4)
                    O16 = O[:].rearrange("p (m r) -> p r m", r=16)
                    ulist = []
                    for uu in range(2):
                        qk = []
                        pv = []
                        for qq in range(2):
                            m = 4 * T + 2 * uu + qq
                            off = qq * 256
                            qs = qT[rows, m * 128:(m + 1) * 128]
                            qk.append((off, 128, kT_[rows, m * 128:(m + 1) * 128], qs, [f"q{m // 4}", f"{kn}{m // 4}", kz]))
                            oc = O[:, (m - 4 * T) * 128:(m - 4 * T + 1) * 128]
                            pv.append((oc, Vt[:, m, vc], off, 128, [f"V0_{m // 8}"]))
                            if m > 0:
                                qk.append((off + 128, 128, kT_[rows, (m - 1) * 128:m * 128], qs,
                                           [f"q{m // 4}", f"{kn}{(m - 1) // 4}"]))
                                pv.append((oc, Vt[:, m - 1, vc], off + 128, 128, [f"V0_{(m - 1) // 8}"]))
                        Eap = Et[:, hd, 0:256].unsqueeze(1).to_broadcast([128, 2, 256])
                        ulist.append(dict(qk=qk, pv=pv, E=Eap, kk=128, eshape=(2, 256)))
                    for uu in range(2):
                        qk = []
                        pv = []
                        for qq in range(2):
                            r = 2 * uu + qq
                            off = qq * 256
                            qs = q4[rows, r, T * 128:(T + 1) * 128]
                            qk.append((off, 128, k4[rows, r, T * 128:(T + 1) * 128], qs, [f"q{T}", f"{kn}{T}"]))
                            oc = O4[:, r, :]
                            blk = 16 + r * 4 + T
                            pv.append((oc, Vt[:, blk, vc], off, 128, [f"V1_{(blk - 16) // 8}"]))
                            if T > 0:
                                qk.append((off + 128, 128, k4[rows, r, (T - 1) * 128:T * 128], qs, [f"q{T}", f"{kn}{T - 1}"]))
                                pv.append((oc, Vt[:, blk - 1, vc], off + 128, 128, [f"V1_{(blk - 17) // 8}"]))
                        Eap = Et[:, hd, 256:512].unsqueeze(1).to_broadcast([128, 2, 256])
                        ulist.append(dict(qk=qk, pv=pv, E=Eap, kk=128, eshape=(2, 256)))
                    kk = 128
                    krows = slice(0, 128)
                    qk = []
                    pv = []
                    for r in range(16):
                        qk.append((r * 32, 32, k16[krows, r, 0:kk], q16[rows, r, 32 * T:32 * T + 32],
                                   [f"q{T}"] + [f"{kn}{tt}" for tt in range(NT)]))
                        pv.append((O16[:, r, :], Vt[0:kk, 32 + r, vc], r * 32, 32, [f"V2_{r // 8}"]))
                    Eap = Et[0:kk, hd, 512 + 32 * T:512 + 32 * T + 32].unsqueeze(1).to_broadcast([kk, 16, 32])
                    ulist.append(dict(qk=qk, pv=pv, E=Eap, kk=kk, eshape=(16, 32)))
                    for ui, u in enumerate(ulist):
                        u["T"] = T
                        u["hd"] = hd
                        u["obank"] = obank
                        u["first"] = (ui == 0)
                        u["last"] = (ui == len(ulist) - 1)
                        units.append(u)

            def emit_qk(u, ui):
                sb = ui % 4
                u["sb"] = sb
                Sps = self.pb[sb]
                kk = u["kk"]
                for (off, n, lh, rh, rd) in u["qk"]:
                    self.mm(Sps[0:kk, off:off + n], lh, rh, True, True, reads=rd, writes=[f"pb{sb}"])
                pi = ui % NPP
                u["pi"] = pi
                p1i = ui % NP1
                p1 = P1[p1i]
                S.op("act", lambda e, p1=p1, Sps=Sps, kk=kk: e.activation(out=p1[0:kk, :], in_=Sps[0:kk, :], func=AF.Exp),
                     reads=[f"pb{sb}"], writes=[f"P1_{p1i}"])
                a, b = u["eshape"]
                pp = PP[pi]
                ppv = pp[0:kk, :].rearrange("p (a b) -> p a b", a=a)
                p1v = p1[0:kk, :].rearrange("p (a b) -> p a b", a=a)
                Eap = u["E"]
                eng = "pool" if ui % 2 == 0 else "dve"
                S.op(eng, lambda e, ppv=ppv, p1v=p1v, Eap=Eap: e.tensor_tensor(out=ppv, in0=p1v, in1=Eap, op=ALU.mult),
                     reads=[f"P1_{p1i}", "E"], writes=[f"P_{pi}"])

            def emit_pv(u):
                ob = u["obank"]
                pp = PP[u["pi"]]
                kk = u["kk"]
                npv = len(u["pv"])
                for i, (oc, lh, off, n, rd) in enumerate(u["pv"]):
                    st = u["first"] and i == 0
                    sp_ = u["last"] and i == npv - 1
                    self.mm(oc, lh, pp[0:kk, off:off + n], st, sp_, reads=rd + [f"P_{u['pi']}", "Vones"],
                            writes=[f"pb{ob}"], skip=True)
                if u["last"]:
                    T, hd = u["T"], u["hd"]
                    O = self.pb[ob]
                    rc = self.tmp[(T % 2) * 2 + hd]
                    rn = f"tmp{(T % 2) * 2 + hd}"
                    if hd == 0:
                        den, num, orow = O[64:128, :], O[0:64, :], slice(0, 64)
                    else:
                        den, num, orow = O[0:64, :], O[64:128, :], slice(64, 128)
                    rcv = rc[orow, :]
                    S.op("act", lambda e, rcv=rcv, den=den: e.activation(out=rcv, in_=den, func=AF.Ln),
                         reads=[f"pb{ob}"], writes=[rn])
                    S.op("act", lambda e, rcv=rcv: e.activation(out=rcv, in_=rcv, func=AF.Exp, scale=-1.0),
                         reads=[rn], writes=[rn])
                    oo = self.os_(hp, T)[orow, :]
                    S.op("dve", lambda e, oo=oo, num=num, rcv=rcv: e.tensor_tensor(out=oo, in0=num, in1=rcv, op=ALU.mult),
                         reads=[f"pb{ob}", rn], writes=[f"o{hp}_{T}"])

            SKEW = 4
            nu = len(units)
            for i in range(nu + SKEW):
                if i < nu:
                    emit_qk(units[i], i)
                if i < 5:
                    emit_tr(tr_groups[i + 1][0], tr_groups[i + 1][1], (i + 1) % 2)
                if i >= SKEW:
                    emit_pv(units[i - SKEW])

    def build(self):
        self.plan_weights()
        self.prologue()
        self._issue_w(0)
        self.wissued = 1
        first = True
        for l in self.layers:
            if l % 2 == 1:
                self.run_side(1000)
                self.attn_prep()
            self.norm(2 * l)
            if first and any(ll % 2 == 1 for ll in self.layers):
                self.build_tables()
                if l % 2 == 1:
                    self.run_side(1000)
            first = False
            if l % 2 == 0:
                self.conv_mixer(l // 2)
            else:
                self.attn_mixer()
            self.out_proj()
            self.norm(2 * l + 1)
            self.ffn()
        if self.final:
            self.norm(8, final=True)
        S = self.S
        keys = self.out_keys
        if not self.final:
            for c in range(KC):
                src = self.xT[:, c * SEQ:(c + 1) * SEQ]
                dst = self.yd[:, c * SEQ:(c + 1) * SEQ]
                S.dma("sp", lambda e, dst=dst, src=src: e.dma_start(out=dst, in_=src),
                      reads=[f"x{c}_{t}" for t in range(NT)], sem_key=f"yo{c}")
                keys.append(f"yo{c}")
        S.emit(self.nc, final_wait_keys=keys)
        return self.nc


def _t5_bucket(dist):
    exact = 16
    df = np.maximum(dist, 1).astype(np.float32)
    large = exact + (np.log(df / np.float32(exact)) / np.float32(math.log(2048 / exact))
                     * np.float32(32 - exact)).astype(np.int32)
    large = np.minimum(large, 31)
    return np.where(dist < exact, dist, large)


def _const_tables():
    oh = np.zeros((33, 3, GL), np.float32)
    for br, dil in enumerate((1, 4, 16)):
        for u in range(GL):
            rel = u - 127
            if 0 <= rel <= 128:
                b = int(_t5_bucket(np.array([rel * dil], np.int32))[0])
                oh[b, br, u] = 1.0
            else:
                oh[32, br, u] = -30000.0
    return oh.reshape(33, 3 * GL)


def _kmaj(w):
    K = w.shape[0] // 128
    return w.reshape(K, 128, w.shape[1]).transpose(1, 0, 2)


def prep_shared(inp, layers=(0, 1, 2, 3)):
    m = {}
    g = np.concatenate([np.stack([inp["mix_norm"][l], inp["ffn_norm"][l]]) for l in range(NL)]
                       + [inp["final_norm"][None]], axis=0)
    m["gains"] = np.ascontiguousarray(g.reshape(9, 8, 128).transpose(2, 0, 1).reshape(128, 72))
    ck = inp["conv_kernel"]
    m["ck"] = np.ascontiguousarray(ck.reshape(2, 3, 8, 128).transpose(3, 0, 2, 1).reshape(128, 48))
    m["ident"] = np.eye(128, dtype=np.float32)
    m["oh"] = _const_tables()
    m["rbx"] = np.concatenate([inp["rel_bias"], np.ones((1, 16), np.float32)], axis=0)
    for l in layers:
        j = l // 2
        wi = inp["conv_w_in"][j] if l % 2 == 0 else inp["attn_w_qkv"][j]
        wo = inp["conv_w_out"][j] if l % 2 == 0 else inp["attn_w_out"][j]
        a = _kmaj(wi).reshape(128, 8, 3, 8, 128)
        m[f"wmix{l}"] = np.ascontiguousarray(a.transpose(3, 0, 1, 2, 4).reshape(8, 128, 3072))
        a = _kmaj(wo).reshape(128, 8, 4, 2, 128)
        m[f"wout{l}"] = np.ascontiguousarray(a.transpose(2, 0, 3, 1, 4).reshape(4, 128, 2048))
        ag = _kmaj(inp["ffn_w_gate"][l]).reshape(128, 8, 22, 128)
        au = _kmaj(inp["ffn_w_up"][l]).reshape(128, 8, 22, 128)
        a = np.stack([ag, au], axis=2)
        m[f"wgu{l}"] = np.ascontiguousarray(a.transpose(3, 0, 1, 2, 4).reshape(22, 128, 2048))
        a = _kmaj(inp["ffn_w_down"][l]).reshape(128, 2, FG, 4, 2, 128)
        m[f"wdn{l}"] = np.ascontiguousarray(a.transpose(1, 3, 0, 4, 2, 5).reshape(8, 128, 2816))
    return m


def x_to_dev(xb):
    return np.ascontiguousarray(xb.T.reshape(8, 128, SEQ).transpose(1, 0, 2).reshape(128, 8 * SEQ))


def y_from_dev(y):
    return np.ascontiguousarray(y.reshape(128, 8, SEQ).transpose(1, 0, 2).reshape(D, SEQ).T)


def kernel(**inputs):
    inp = {k: np.asarray(v, dtype=np.float32) for k, v in inputs.items()}
    shared = prep_shared(inp)
    nc = Builder().build()
    nb = inp["x"].shape[0]
    in_maps = []
    for b in range(nb):
        m = dict(shared)
        m["x"] = x_to_dev(inp["x"][b])
        in_maps.append(m)
    res = run_bass_kernel_spmd(nc, in_maps, core_ids=list(range(nb)))
    out = np.stack([y_from_dev(np.asarray(res.results[b]["y"], dtype=np.float32)) for b in range(nb)], axis=0)
    return out.astype(np.float32)
```
